# Optimizing a Trainium2 kernel written in Bass

```python
import math
import jax, jax.numpy as jnp
from jax import lax
import numpy as np

D_MODEL = 2048
BATCH = 4
SEQ = 2048
DEPTH = 2

CTX_LEN = 256
GRID_W = 64
Q_BLOCK = 128
EPS = 1e-6
ROPE_THETA = 10000.0

DA_HEADS = 6
DA_QK_DIM = 64
DA_V_DIM = 2 * DA_QK_DIM
DA_WIDTH = DA_HEADS * DA_V_DIM
DA_QK_COLS = DA_HEADS * 2 * DA_QK_DIM
DA_SCALE = DA_QK_DIM ** -0.5

GM_GROUPS = 4
GM_CH = 128
GM_CHUNK = 128
GM_WIDTH = GM_GROUPS * GM_CH

MLA_HEADS = 6
MLA_Q_RANK = 512
MLA_KV_RANK = 512
MLA_NOPE = 128
MLA_ROPE = 64
MLA_V = 128
MLA_WIDTH = MLA_HEADS * MLA_V
MLA_SCALE = (MLA_NOPE + MLA_ROPE) ** -0.5

ROT_DIM = 64
MIX_WIDTH = DA_WIDTH + GM_WIDTH + MLA_WIDTH
IN_SIZES = (DA_QK_COLS, DA_QK_COLS, DA_WIDTH, GM_WIDTH, GM_WIDTH, MLA_Q_RANK, MLA_KV_RANK, MLA_ROPE)
IN_COLS = sum(IN_SIZES)
D_FF = 4 * D_MODEL

kernel_name = 'hybrid_diffattn_gmlp_mla_dit'


def rms_norm(x, g):
    xf = x.astype(jnp.float32)
    y = xf * lax.rsqrt(jnp.mean(xf * xf, axis=-1, keepdims=True) + EPS)
    return (y * g.astype(jnp.float32)).astype(x.dtype)


def modulate(h, shift, scale):
    return h * (1.0 + scale) + shift


def axial_rope_tables(rows):
    row = jnp.repeat(jnp.arange(rows, dtype=jnp.float32), GRID_W)
    col = jnp.tile(jnp.arange(GRID_W, dtype=jnp.float32), rows)
    n_f = ROT_DIM // 4
    inv = ROPE_THETA ** (-jnp.arange(n_f, dtype=jnp.float32) / n_f)
    ang = jnp.concatenate([row[:, None] * inv, col[:, None] * inv], axis=-1)
    return jnp.cos(ang), jnp.sin(ang)


def apply_rope(x, cos, sin):
    extra = x.ndim - 3
    cs = cos.reshape(cos.shape[:1] + (1,) * extra + cos.shape[1:]).astype(x.dtype)
    sn = sin.reshape(sin.shape[:1] + (1,) * extra + sin.shape[1:]).astype(x.dtype)
    xp = x.reshape(x.shape[:-1] + (x.shape[-1] // 2, 2))
    x0, x1 = xp[..., 0], xp[..., 1]
    return jnp.stack([x0 * cs - x1 * sn, x0 * sn + x1 * cs], axis=-1).reshape(x.shape)


def split_in(z):
    offs = [int(o) for o in np.cumsum(IN_SIZES)[:-1]]
    return jnp.split(z, offs, axis=-1)


def over_query_blocks(fn, *qs):
    b, n = qs[0].shape[:2]
    nb = n // Q_BLOCK
    blocks = tuple(jnp.moveaxis(q.reshape((b, nb, Q_BLOCK) + q.shape[2:]), 1, 0) for q in qs)
    out = lax.map(lambda blk: fn(*blk), blocks)
    out = jnp.moveaxis(out, 0, 1)
    return out.reshape((b, n) + out.shape[3:])


def diff_attention(q, k, v, lam, scale):
    s = jnp.einsum('bqhcd,bkhcd->bhcqk', q, k).astype(jnp.float32) * scale
    p = jax.nn.softmax(s, axis=-1)
    w = p[:, :, 0] - lam * p[:, :, 1]
    return jnp.einsum('bhqk,bkhd->bqhd', w.astype(v.dtype), v)


def mla_attention(q_nope, q_rope, k_nope, k_rope, v, scale):
    s = (jnp.einsum('bqhd,bkhd->bhqk', q_nope, k_nope)
         + jnp.einsum('bqhr,bkr->bhqk', q_rope, k_rope)).astype(jnp.float32) * scale
    p = jax.nn.softmax(s, axis=-1).astype(v.dtype)
    return jnp.einsum('bhqk,bkhd->bqhd', p, v)


def chunk_spatial_gating(u, v, w_spatial, b_spatial):
    b, n = u.shape[:2]
    vb = v.reshape(b, n // GM_CHUNK, GM_CHUNK, GM_GROUPS, GM_CH)
    mixed = jnp.einsum('gpq,bmqgc->bmpgc', w_spatial, vb) + b_spatial.T[None, None, :, :, None]
    return (u * mixed.reshape(b, n, GM_GROUPS, GM_CH)).reshape(b, n, GM_WIDTH)


def keys_values(z, g_mla_kv, w_mla_ukv, rope):
    b, n, _ = z.shape
    _, k, v, _, _, _, ckv, kr = split_in(z)
    k = k.reshape(b, n, DA_HEADS, 2, DA_QK_DIM)
    v = v.reshape(b, n, DA_HEADS, DA_V_DIM)
    kv = (rms_norm(ckv, g_mla_kv) @ w_mla_ukv).reshape(b, n, MLA_HEADS, MLA_NOPE + MLA_V)
    k_nope, v_mla = kv[..., :MLA_NOPE], kv[..., MLA_NOPE:]
    if rope is not None:
        cos, sin = rope
        k = apply_rope(k, cos, sin)
        kr = apply_rope(kr, cos, sin)
    return k, v, k_nope, kr, v_mla


def queries_and_gating(z, g_gm_v, w_spatial, b_spatial, g_mla_q, w_mla_uq, rope):
    b, n, _ = z.shape
    q, _, _, gu, gv, cq, _, _ = split_in(z)
    q = q.reshape(b, n, DA_HEADS, 2, DA_QK_DIM)
    qm = (rms_norm(cq, g_mla_q) @ w_mla_uq).reshape(b, n, MLA_HEADS, MLA_NOPE + MLA_ROPE)
    q_nope, q_rope = qm[..., :MLA_NOPE], qm[..., MLA_NOPE:]
    if rope is not None:
        cos, sin = rope
        q = apply_rope(q, cos, sin)
        q_rope = apply_rope(q_rope, cos, sin)
    u = jax.nn.gelu(gu, approximate=False).reshape(b, n, GM_GROUPS, GM_CH)
    v = rms_norm(jax.nn.gelu(gv, approximate=False).reshape(b, n, GM_GROUPS, GM_CH), g_gm_v)
    gated = chunk_spatial_gating(u, v, w_spatial, b_spatial)
    return q, q_nope, q_rope, gated


def diff_head_out(o, g_da_sub, lam_init):
    b, n = o.shape[:2]
    return (rms_norm(o, g_da_sub) * (1.0 - lam_init)).reshape(b, n, DA_WIDTH)


def sq_relu_mlp(h, w_fc1, w_fc2):
    return jnp.square(jax.nn.relu(h @ w_fc1)) @ w_fc2


def setup_inputs(seed: int = 0) -> dict:
    key = jax.random.key(seed)
    ks = jax.random.split(key, 25)
    f32 = jnp.float32

    def nrm(k, shape, scale):
        return jax.random.normal(k, shape, f32) * scale

    def gain(k, shape):
        return 1.0 + 0.02 * jax.random.normal(k, shape, f32)

    return {
        'x': nrm(ks[0], (BATCH, SEQ, D_MODEL), 1.0),
        'c': nrm(ks[1], (BATCH, D_MODEL), 1.0),
        'ctx': nrm(ks[2], (BATCH, CTX_LEN, D_MODEL), 1.0),
        'c_ctx': nrm(ks[3], (D_MODEL,), 1.0),
        'w_mod': nrm(ks[4], (DEPTH, D_MODEL, 6 * D_MODEL), 0.5 * D_MODEL ** -0.5),
        'b_mod': nrm(ks[5], (DEPTH, 6 * D_MODEL), 0.01),
        'g_norm_mix': gain(ks[6], (DEPTH, D_MODEL)),
        'g_norm_mlp': gain(ks[7], (DEPTH, D_MODEL)),
        'w_in': nrm(ks[8], (DEPTH, D_MODEL, IN_COLS), D_MODEL ** -0.5),
        'lam_q1': nrm(ks[9], (DEPTH, DA_QK_DIM), 0.1),
        'lam_k1': nrm(ks[10], (DEPTH, DA_QK_DIM), 0.1),
        'lam_q2': nrm(ks[11], (DEPTH, DA_QK_DIM), 0.1),
        'lam_k2': nrm(ks[12], (DEPTH, DA_QK_DIM), 0.1),
        'g_da_sub': gain(ks[13], (DEPTH, DA_V_DIM)),
        'g_gm_v': gain(ks[14], (DEPTH, GM_GROUPS, GM_CH)),
        'w_spatial': nrm(ks[15], (DEPTH, GM_GROUPS, GM_CHUNK, GM_CHUNK), GM_CHUNK ** -0.5),
        'b_spatial': gain(ks[16], (DEPTH, GM_GROUPS, GM_CHUNK)),
        'g_mla_q': gain(ks[17], (DEPTH, MLA_Q_RANK)),
        'w_mla_uq': nrm(ks[18], (DEPTH, MLA_Q_RANK, MLA_HEADS * (MLA_NOPE + MLA_ROPE)), MLA_Q_RANK ** -0.5),
        'g_mla_kv': gain(ks[19], (DEPTH, MLA_KV_RANK)),
        'w_mla_ukv': nrm(ks[20], (DEPTH, MLA_KV_RANK, MLA_HEADS * (MLA_NOPE + MLA_V)), MLA_KV_RANK ** -0.5),
        'w_out': nrm(ks[21], (DEPTH, MIX_WIDTH, D_MODEL), MIX_WIDTH ** -0.5),
        'w_fc1': nrm(ks[22], (DEPTH, D_MODEL, D_FF), D_MODEL ** -0.5),
        'w_fc2': nrm(ks[23], (DEPTH, D_FF, D_MODEL), D_FF ** -0.5),
        'g_final': gain(ks[24], (D_MODEL,)),
    }


def reference(x, c, ctx, c_ctx, w_mod, b_mod, g_norm_mix, g_norm_mlp, w_in,
              lam_q1, lam_k1, lam_q2, lam_k2, g_da_sub, g_gm_v, w_spatial, b_spatial,
              g_mla_q, w_mla_uq, g_mla_kv, w_mla_ukv, w_out, w_fc1, w_fc2, g_final):
    n = x.shape[1]
    rows = n // GRID_W
    rope = axial_rope_tables(rows)
    xc = ctx
    sc_lat = jax.nn.silu(c)
    sc_ctx = jax.nn.silu(c_ctx)

    for l in range(DEPTH):
        update_ctx = l < DEPTH - 1
        lam_init = 0.8 - 0.6 * math.exp(-0.3 * l)
        f32 = jnp.float32
        lam = (jnp.exp(jnp.sum(lam_q1[l].astype(f32) * lam_k1[l].astype(f32)))
               - jnp.exp(jnp.sum(lam_q2[l].astype(f32) * lam_k2[l].astype(f32))) + lam_init)

        mod = sc_lat @ w_mod[l] + b_mod[l]
        mod_c = sc_ctx @ w_mod[l] + b_mod[l]
        sh1, sc1, gt1, sh2, sc2, gt2 = jnp.split(mod[:, None, :], 6, axis=-1)
        csh1, csc1, cgt1, csh2, csc2, cgt2 = jnp.split(mod_c, 6, axis=-1)

        h = modulate(rms_norm(x, g_norm_mix[l]), sh1, sc1)
        hc = modulate(rms_norm(xc, g_norm_mix[l]), csh1, csc1)
        z = h @ w_in[l]
        zc = hc @ w_in[l]

        k, v, kn, kr, vm = keys_values(z, g_mla_kv[l], w_mla_ukv[l], rope)
        kc, vc, knc, krc, vmc = keys_values(zc, g_mla_kv[l], w_mla_ukv[l], None)
        q, qn, qr, gated = queries_and_gating(z, g_gm_v[l], w_spatial[l], b_spatial[l],
                                              g_mla_q[l], w_mla_uq[l], rope)

        k_all = jnp.concatenate([kc, k], axis=1)
        v_all = jnp.concatenate([vc, v], axis=1)
        kn_all = jnp.concatenate([knc, kn], axis=1)
        kr_all = jnp.concatenate([krc, kr], axis=1)
        vm_all = jnp.concatenate([vmc, vm], axis=1)
        o_da = over_query_blocks(lambda qb: diff_attention(qb, k_all, v_all, lam, DA_SCALE), q)
        o_mla = over_query_blocks(
            lambda qnb, qrb: mla_attention(qnb, qrb, kn_all, kr_all, vm_all, MLA_SCALE), qn, qr)
        heads = jnp.concatenate([diff_head_out(o_da, g_da_sub[l], lam_init), gated,
                                 o_mla.reshape(o_mla.shape[0], n, MLA_WIDTH)], axis=-1)
        x = x + gt1 * (heads @ w_out[l])

        h2 = modulate(rms_norm(x, g_norm_mlp[l]), sh2, sc2)
        x = x + gt2 * sq_relu_mlp(h2, w_fc1[l], w_fc2[l])

        if update_ctx:
            qc, qnc, qrc, gated_c = queries_and_gating(zc, g_gm_v[l], w_spatial[l], b_spatial[l],
                                                       g_mla_q[l], w_mla_uq[l], None)
            o_da_c = diff_attention(qc, kc, vc, lam, DA_SCALE)
            o_mla_c = mla_attention(qnc, qrc, knc, krc, vmc, MLA_SCALE)
            heads_c = jnp.concatenate([diff_head_out(o_da_c, g_da_sub[l], lam_init), gated_c,
                                       o_mla_c.reshape(o_mla_c.shape[0], CTX_LEN, MLA_WIDTH)], axis=-1)
            xc = xc + cgt1 * (heads_c @ w_out[l])
            h2c = modulate(rms_norm(xc, g_norm_mlp[l]), csh2, csc2)
            xc = xc + cgt2 * sq_relu_mlp(h2c, w_fc1[l], w_fc2[l])

    return rms_norm(x, g_final)
```

```python
import math
import numpy as np
import concourse.bass as bass
import concourse.mybir as mybir
from concourse.bass_utils import run_bass_kernel_spmd

F32 = mybir.dt.float32
BF16 = mybir.dt.bfloat16
AF = mybir.ActivationFunctionType
ALU = mybir.AluOpType
AX = mybir.AxisListType

D = 2048
KC = 16
NTOK = 2304
NQ = 1280
CTX = 256
OWN = 1024
DFF = 8192
EPS = 1e-6
DA_SCALE = 64 ** -0.5
MLA_SCALE = 192 ** -0.5
IN_COLS = 4416
C_Q, C_K, C_V, C_GU, C_GV, C_CQ, C_CKV, C_KR = 0, 768, 1536, 2304, 2816, 3328, 3840, 4352

ENGS = ("pe", "act", "dve", "pool", "sp")


def I(method, *a, **kw):
    return lambda e: getattr(e, method)(*a, **kw)


class T:
    __slots__ = ("name", "buf", "last_w", "readers")

    def __init__(self, name, buf):
        self.name = name
        self.buf = buf
        self.last_w = None
        self.readers = {}


class Buf:
    def __init__(self, S, name, space, start, nbytes, h=None):
        self.S = S
        self.name = name
        self.space = space
        self.start = start
        self.end = start + nbytes
        self.h = h
        self.tiles = {}
        self.overlaps = []
        self.stamp = -1
        self.acq = -1
        self.pending = []
        for o in S.bufs:
            if o.space == space and o.start < self.end and self.start < o.end:
                o.overlaps.append(self)
                self.overlaps.append(o)
        S.bufs.append(self)

    def t(self, key=0):
        tt = self.tiles.get(key)
        if tt is None:
            tt = T(f"{self.name}[{key}]", self)
            for i, p in enumerate(self.pending):
                tt.readers[("p", i)] = p
            self.tiles[key] = tt
        return tt

    def outstanding(self):
        out = []
        for tt in self.tiles.values():
            if tt.last_w is not None:
                out.append(tt.last_w)
            out.extend(tt.readers.values())
        return out


class Op:
    __slots__ = ("eng", "fn", "reads", "writes", "ndma", "semkey", "deps", "signal", "count", "name")

    def __init__(self, eng, fn, reads, writes, ndma, semkey, name):
        self.eng = eng
        self.fn = fn
        self.reads = reads
        self.writes = writes
        self.ndma = ndma
        self.semkey = semkey
        self.deps = []
        self.signal = False
        self.count = None
        self.name = name


class Sched:
    def __init__(self, nc):
        self.nc = nc
        self.ops = {e: [] for e in ENGS}
        self.all_ops = []
        self.bufs = []
        self.now = 0

    def _touch(self, B):
        if B.overlaps:
            for Y in B.overlaps:
                if Y.stamp > B.acq:
                    pend = Y.outstanding()
                    if pend:
                        base = len(B.pending)
                        B.pending = B.pending + pend
                        for tt in B.tiles.values():
                            for i, p in enumerate(pend):
                                tt.readers[("p", base + i)] = p
            B.acq = self.now
        B.stamp = self.now

    def op(self, eng, fn, reads=(), writes=(), ndma=0, semkey=None, name=""):
        self.now += 1
        o = Op(eng, fn, list(reads), list(writes), ndma, semkey, name)
        seen = set()
        for t in o.reads + o.writes:
            if id(t.buf) not in seen:
                seen.add(id(t.buf))
                self._touch(t.buf)
        deps = {}
        for t in o.reads:
            if t.last_w is not None:
                deps[id(t.last_w)] = t.last_w
        for t in o.writes:
            if t.last_w is not None:
                deps[id(t.last_w)] = t.last_w
            for r in t.readers.values():
                deps[id(r)] = r
        deps.pop(id(o), None)
        o.deps = list(deps.values())
        rkey = ("d", self.now) if ndma else eng
        for t in o.reads:
            t.readers[rkey] = o
        for t in o.writes:
            t.last_w = o
            t.readers = {}
        self.ops[eng].append(o)
        self.all_ops.append(o)
        return o

    def dma(self, eng, pairs, reads=(), writes=(), semkey=None, name=""):
        def fn(e, pairs=pairs):
            return [e.dma_start(out=o_, in_=i_) for (o_, i_) in pairs]
        return self.op(eng, fn, reads, writes, ndma=len(pairs), semkey=semkey, name=name)

    def final_wait(self, eng, tiles):
        return self.op(eng, lambda e: None, reads=tiles, name="final_wait")

    @staticmethod
    def _is_raw(o, d):
        ws = set(id(t) for t in d.writes)
        return any(id(t) in ws for t in o.reads)

    @staticmethod
    def _is_war(o, d):
        rs = set(id(t) for t in d.reads)
        return any(id(t) in rs for t in o.writes)

    def _needs_sem(self, o, d):
        if d.ndma:
            return True
        if d.eng == o.eng and not o.ndma:
            if d.eng == "pe":
                return False
            return self._is_raw(o, d) or self._is_war(o, d)
        return True

    def emit(self):
        nc = self.nc
        for o in self.all_ops:
            for d in o.deps:
                if not d.ndma and self._needs_sem(o, d):
                    d.signal = True
        semkeys = {}
        for e in ENGS:
            c = 0
            for o in self.ops[e]:
                if o.ndma:
                    semkeys[o.semkey] = semkeys.get(o.semkey, 0) + 16 * o.ndma
                    o.count = semkeys[o.semkey]
                elif o.signal:
                    c += 1
                    o.count = c
        sems = {}
        for e in ENGS:
            sems[("eng", e)] = nc.alloc_semaphore(name=f"s_{e}")
        for k in semkeys:
            sems[("dma", k)] = nc.alloc_semaphore(name=f"d_{k}")
        engobj = {"pe": nc.tensor, "act": nc.scalar, "dve": nc.vector, "pool": nc.gpsimd, "sp": nc.sync}
        self.nwaits = 0
        self.ninst = 0
        for e in ENGS:
            eng = engobj[e]
            known = {}
            for o in self.ops[e]:
                need = {}
                for d in o.deps:
                    if not self._needs_sem(o, d):
                        continue
                    key = ("dma", d.semkey) if d.ndma else ("eng", d.eng)
                    val = d.count
                    if known.get(key, 0) >= val:
                        continue
                    if need.get(key, 0) < val:
                        need[key] = val
                for key, val in need.items():
                    eng.wait_ge(sems[key], val)
                    known[key] = val
                    self.nwaits += 1
                r = o.fn(eng)
                if r is None:
                    continue
                self.ninst += 1
                if o.ndma:
                    for ins in r:
                        ins.then_inc(sems[("dma", o.semkey)], 16)
                elif o.signal:
                    r.then_inc(sems[("eng", e)], 1)


SB_BASE = 16512
SB_END = 229344
A0, B0, C0, D0 = SB_BASE, SB_BASE + 73728, SB_BASE + 114688, SB_BASE + 196608


class Prog:
    def __init__(self, layers, fused, dbg=()):
        self.layers = layers
        self.fused = fused
        self.dbg = dbg
        self.nc = bass.Bass("TRN2", target_bir_lowering=False)
        self.S = Sched(self.nc)
        self.dbg_outs = []
        self._n = 0

    def sb(self, name, shape, dtype, off):
        esz = 4 if dtype == F32 else 2
        nbytes = esz * int(np.prod(shape[1:]))
        assert off + nbytes <= SB_END, (name, off, nbytes)
        self._n += 1
        h = self.nc.alloc_sbuf_tensor_at(f"{name}_{self._n}", list(shape), dtype, offset=off)
        return Buf(self.S, name, "sb", off, nbytes, h)

    def dram_in(self, name, shape, dtype=F32):
        h = self.nc.dram_tensor(name, list(shape), dtype, kind="ExternalInput")
        return h.ap()

    def dram_out(self, name, shape, dtype=F32):
        h = self.nc.dram_tensor(name, list(shape), dtype, kind="ExternalOutput")
        b = Buf(self.S, name, "dram:" + name, 0, 1, h.ap())
        return b

    def dram_scratch(self, name, shape, dtype=F32):
        h = self.nc.dram_tensor(name, list(shape), dtype, kind="Internal")
        b = Buf(self.S, name, "dram:" + name, 0, 1, h.ap())
        return b

    def pe(self, fn, reads, writes):
        return self.S.op("pe", fn, reads, writes)

    def act(self, fn, reads, writes):
        return self.S.op("act", fn, reads, writes)

    def dve(self, fn, reads, writes):
        return self.S.op("dve", fn, reads, writes)

    def mm(self, ps_ap, ps_t, pairs, reads):
        n = len(pairs)
        for i, (l, r) in enumerate(pairs):
            self.pe(I("matmul", ps_ap, lhsT=l, rhs=r, start=(i == 0), stop=(i == n - 1)),
                    reads, [ps_t])

    def dump(self, name, buf, ap, tiles, shape, dtype=F32):
        if name not in self.dbg:
            return
        ob = self.dram_out("dbg_" + name, shape, dtype)
        self.S.dma("sp", [(ob.h, ap)], reads=tiles, writes=[ob.t()], semkey="dbg")
        self.dbg_outs.append(ob)

    def build(self):
        nc, S = self.nc, self.S
        L = self.layers
        nl = len(L)
        self.xin = self.dram_in("xin", [D, NTOK])
        self.smalls_d = self.dram_in("smalls", [128, self.n_smalls()])
        self.mats_d = self.dram_in("mats", [128, 384])
        self.tab_d = self.dram_in("ropetab", [128, 2, 2048])
        self.wspt_d = self.dram_in("wspt", [nl, 128, 4, 128])
        self.bc_d = self.dram_in("bcast", [nl, 128, 1280])
        self.w_mod = self.dram_in("w_mod", [nl, D, 6 * D])
        self.w_in = self.dram_in("w_in", [nl, D, IN_COLS])
        self.w_uq = self.dram_in("w_mla_uq", [nl, 512, 1152])
        self.w_ukv = self.dram_in("w_mla_ukv", [nl, 512, 1536])
        self.w_out = self.dram_in("w_out", [nl, D, D])
        self.w_fc1 = self.dram_in("w_fc1", [nl, D, DFF])
        self.w_fc2 = self.dram_in("w_fc2", [nl, DFF, D])
        last = (L[-1] == 1)
        if last:
            self.out_b = self.dram_out("outT", [D, OWN])
        else:
            self.out_b = self.dram_out("xs_out", [D, NQ])
        self.P = []
        for i in range(8):
            h = nc.alloc_psum_tensor(f"ps{i}", [128, 512], F32)
            self.P.append(Buf(S, f"P{i}", f"ps{i}", 0, 1, h))
        o = D0
        ns = self.n_smalls()
        self.SM = self.sb("smalls", [128, ns], F32, o); o += (4 * ns + 31) // 32 * 32
        self.MATS = self.sb("mats", [128, 384], F32, o); o += 1536
        self.IDB = self.sb("identb", [128, 128], BF16, o); o += 256
        self.ONB = self.sb("onesb", [128, 128], BF16, o); o += 256
        self.SIL = self.sb("sil", [128, 16, 2], BF16, o); o += 64
        self.MODV = [self.sb(f"modv{i}", [128, 96, 2], F32, o + 768 * i) for i in range(2)]; o += 1536
        self.GS = [self.sb(f"gs{i}", [128, 2, 16, 2], F32, o + 256 * i) for i in range(2)]; o += 512
        self.LAMV = self.sb("lamv", [128, 8], F32, o); o += 32
        self.WSPT = self.sb("wsptb", [128, 4, 128], BF16, o); o += 1024
        self.BC = self.sb("bc", [128, 1280], F32, o); o += 5120
        self.LTMP = self.sb("ltmp", [128, 2, 64], F32, o); o += 512
        self.RTB = self.sb("rtb", [128, 128], BF16, o); o += 256
        assert o <= SB_END, o
        self.H = self.sb("H", [128, 16, NTOK], BF16, A0)
        self.H2 = self.sb("H2", [128, 16, NQ], BF16, A0)
        self.AJ = self.sb("AJ", [128, 8, NQ], BF16, A0 + 40960)
        self.SQ2 = [self.sb(f"SQ2_{i}", [128, 512], F32, A0 + 61440 + 2048 * i) for i in range(3)]
        self.TMPR = [self.sb(f"TMPR{i}", [128, 512], F32, A0 + 67584 + 2048 * i) for i in range(2)]
        self.QN = self.sb("QN", [128, NQ], BF16, A0)
        self.QR = self.sb("QR", [128, NQ], BF16, A0 + 2560)
        self.KN = self.sb("KN", [128, NTOK], BF16, A0 + 5120)
        self.VMT = self.sb("VMT", [128, NTOK], BF16, A0 + 9728)
        self.VM = self.sb("VM", [128, 18, 128], BF16, A0 + 14336)
        self.EM = [self.sb(f"EM{i}", [128, 512], BF16, A0 + 18944 + 1024 * i) for i in range(4)]
        self.EPM = [self.sb(f"EPM{i}", [128, 512], F32, A0 + 23040 + 2048 * i) for i in range(2)]
        self.RA = [self.sb(f"RA{i}", [128, 4096], BF16, A0 + 28672 + 8192 * i) for i in range(3)]
        self.XOLD = [self.sb(f"XOLD{i}", [128, NQ], F32, A0 + 53248 + 5120 * i) for i in range(2)]
        self.HD = self.sb("HD", [128, 16, NQ], BF16, B0)
        self.RB = [self.sb(f"RB{i}", [128, 4096], BF16, B0 + 8192 * i) for i in range(3)]
        self.RS2 = self.sb("RS2", [128, 512], F32, B0 + 24576)
        self.N2T = [self.sb(f"N2T{i}", [128, 512], F32, B0 + 26624 + 2048 * i) for i in range(2)]
        self.OST = [self.sb(f"OST{i}", [128, 512], F32, B0 + 30720 + 2048 * i) for i in range(2)]
        self.CQF = self.sb("CQF", [128, 4, 512], F32, B0 + 25600)
        self.SQT = [self.sb(f"SQT{i}", [128, 512], F32, B0 + 33792 + 2048 * i) for i in range(2)]
        self.RSM = self.sb("RSM", [128, 512], F32, B0 + 37888)
        self.X = self.sb("X", [128, 16, NQ], F32, C0)
        self.XSEG = [self.sb(f"XSEG{i}", [128, 16, 256], F32, C0 + 16384 * i) for i in range(2)]
        self.SQb = [self.sb(f"SQ{i}", [128, 16, 256], F32, C0 + 32768 + 16384 * i) for i in range(2)]
        self.RSb = [self.sb(f"RS{i}", [128, 256], F32, C0 + 65536 + 1024 * i) for i in range(2)]
        self.RC = [self.sb(f"RC{i}", [128, 4096], BF16, C0 + 8192 * i) for i in range(3)]
        self.TAB = self.sb("TAB", [128, 2, 2048], F32, C0 + 24576)
        c1 = C0 + 40960
        self.QT = self.sb("QT", [128, NQ], BF16, c1)
        self.KT = self.sb("KT", [128, NTOK], BF16, c1 + 2560)
        self.VT = self.sb("VT", [128, NTOK], BF16, c1 + 7168)
        self.V = self.sb("V", [128, 18, 128], BF16, c1 + 11776)
        self.XC = [self.sb(f"XC{i}", [128, 512], BF16, C0 + 57344 + 4096 * i) for i in range(2)]
        self.XS_ = [self.sb(f"XS{i}", [128, 512], BF16, C0 + 57344 + 4096 * i + 2048) for i in range(2)]
        self.ED = [self.sb(f"ED{i}", [128, 512], BF16, C0 + 65536 + 1024 * i) for i in range(4)]
        self.EPD = [self.sb(f"EPD{i}", [128, 512], F32, C0 + 69632 + 2048 * i) for i in range(5)]
        self.U = self.sb("U", [128, 4, 512], F32, c1)
        self.GVT = self.sb("GVT", [128, 4, 512], F32, c1 + 8192)
        self.VTOK = [self.sb(f"VTOK{i}", [128, 512], BF16, c1 + 16384 + 1024 * i) for i in range(2)]
        self.VF = [self.sb(f"VF{i}", [128, 512], F32, c1 + 18432 + 2048 * i) for i in range(2)]
        self.GST = self.sb("GST", [128, 16], F32, c1 + 22528)
        self.MIXT = self.sb("MIXT", [128, 512], F32, c1 + 22592)
        self.MIXT2 = self.sb("MIXT2", [128, 512], F32, c1 + 24640)
        self.CQN = self.sb("CQN", [128, 4, NQ], BF16, c1)
        self.CKVN = self.sb("CKVN", [128, 4, NTOK], BF16, c1 + 10240)
        self.KRT = self.sb("KRT", [128, NTOK], BF16, c1 + 28672)
        self.XC2 = self.sb("XC2", [128, 512], BF16, c1 + 33280)
        self.XS2 = self.sb("XS2", [128, 512], BF16, c1 + 35328)
        assert c1 + 37376 <= C0 + 81920
        if nl == 2:
            self.XSD = self.dram_scratch("xs_scr", [D, NTOK])
            self.KVS = self.dram_scratch("kv_scr", [6, 2, 128, NTOK], BF16)
            self.MKS = self.dram_scratch("mk_scr", [5, 128, NTOK], BF16)

        S.dma("sp", [(self.SM.h[:], self.smalls_d)], writes=[self.SM.t()], semkey="c0")
        S.dma("sp", [(self.MATS.h[:], self.mats_d)], writes=[self.MATS.t()], semkey="c1")
        self.dve(I("tensor_copy", out=self.IDB.h[:], in_=self.MATS.h[:, 0:128]), [self.MATS.t()], [self.IDB.t()])
        self.dve(I("tensor_copy", out=self.ONB.h[:], in_=self.MATS.h[:, 256:384]), [self.MATS.t()], [self.ONB.t()])
        self.dve(I("tensor_copy", out=self.RTB.h[:], in_=self.MATS.h[:, 128:256]), [self.MATS.t()], [self.RTB.t()])
        self.act(I("activation", out=self.SIL.h[:], in_=self.SM.h[:, 0:32].rearrange("p (k c) -> p k c", c=2),
                                        func=AF.Silu), [self.SM.t()], [self.SIL.t()])
        self.ident = self.MATS.h[:, 0:128]
        self.rt = self.MATS.h[:, 128:256]
        self.ones = self.MATS.h[:, 256:384]
        self.ring_i = {"A": 0, "B": 0, "C": 0}
        self.rings = {"A": self.RA, "B": self.RB, "C": self.RC}

        self.mod_ring = "B"
        self.mod_bank = 7
        self.mod_gens = {}
        for li, l in enumerate(L):
            self.mod_gens[li] = self.mod_gen(li)
        self.pump(0, 16)
        qA = [(0, 256, 1), (256, 512, 0), (768, 512, 0)]
        qO = [(256, 512, 0), (768, 512, 0)]
        for li, l in enumerate(L):
            if l == 0:
                self.layer(li, l, qA, 0, xs_cols=(0, NQ), kv=("save" if nl == 2 else "compute"))
                if nl == 2:
                    self.layer(li, l, qO, 1024, xs_cols=(256, NQ), kv="load")
            else:
                self.layer(li, l, qO, 0)
        S.final_wait("sp", list(self.out_b.tiles.values()) + [b.t() for b in self.dbg_outs])
        S.emit()
        return nc

    def n_smalls(self):
        return 32 + len(self.layers) * 137 + 16

    def sm_off(self, li, what):
        base = 32 + li * 137
        offs = {"bmod": 0, "gmix": 96, "gmlp": 112, "gq": 128, "gkv": 132, "gsub": 136}
        return base + offs[what]

    def wload(self, ring, src3d, kc, ncols):
        bufs = self.rings[ring]
        i = self.ring_i[ring]
        self.ring_i[ring] = (i + 1) % len(bufs)
        b = bufs[i]
        view = b.h[:, 0:kc * ncols].rearrange("p (k n) -> p k n", k=kc)
        self.S.dma("pool", [(view, src3d)], writes=[b.t()], semkey=f"w{ring}{i}")
        return view, b.t()

    def wsrc(self, w, li, r0, kc, c0, ncols):
        return w[li, r0:r0 + kc * 128, c0:c0 + ncols].rearrange("(k p) n -> p k n", p=128)

    def mod_gen(self, li):
        MV = self.MODV[li]
        GS = self.GS[li]
        for blk in range(48):
            Pm = self.P[self.mod_bank]
            w, wt = self.wload(self.mod_ring, self.wsrc(self.w_mod, li, 0, 16, blk * 256, 256), 16, 256)
            for jj in range(2):
                j = blk * 2 + jj
                self.mm(Pm.h[:, 2 * j:2 * j + 2], Pm.t(),
                        [(w[:, k, jj * 128:(jj + 1) * 128], self.SIL.h[:, k, :]) for k in range(16)],
                        [wt, self.SIL.t()])
            s = blk // 8
            bo = self.sm_off(li, "bmod")
            self.dve(I("tensor_tensor",
                       out=MV.h[:, 2 * blk:2 * blk + 2, :],
                       in0=Pm.h[:, 4 * blk:4 * blk + 4].rearrange("p (j c) -> p j c", c=2),
                       in1=self.SM.h[:, bo + 2 * blk:bo + 2 * blk + 2].unsqueeze(2).broadcast_to([128, 2, 2]),
                       op=ALU.add), [Pm.t(), self.SM.t()], [MV.t(s)])
            if blk % 8 == 7:
                if s in (1, 4):
                    which = 0 if s == 1 else 1
                    go = self.sm_off(li, "gmix" if s == 1 else "gmlp")
                    self.dve(I("scalar_tensor_tensor",
                        out=GS.h[:, which, :, :], in0=MV.h[:, 16 * s:16 * s + 16, :], scalar=1.0,
                        in1=self.SM.h[:, go:go + 16].unsqueeze(2).broadcast_to([128, 16, 2]),
                        op0=ALU.add, op1=ALU.mult), [MV.t(s), self.SM.t()], [GS.t(which)])
            yield blk

    def pump(self, li, n):
        g = self.mod_gens.get(li)
        if g is None:
            return
        for _ in range(n):
            try:
                next(g)
            except StopIteration:
                self.mod_gens[li] = None
                return

    def modc(self, li, sec, j, c):
        return self.MODV[li].h[:, 16 * sec + j, c:c + 1]

    def rstd_from(self, ps, n, dst, scale):
        self.act(I("activation", out=dst.h[:, 0:n], in_=ps.h[:, 0:n], func=AF.Sqrt, bias=EPS, scale=scale),
                 [ps.t()], [dst.t()])
        self.dve(I("reciprocal", out=dst.h[:, 0:n], in_=dst.h[:, 0:n]), [dst.t()], [dst.t()])

    def rope_evac(self, ps, n, P_, lat0, out_ap, out_t, xc, xs, pr):
        tab = self.TAB
        self.dve(I("tensor_tensor", out=xc.h[0:P_, 0:n], in0=ps.h[0:P_, 0:n], in1=tab.h[0:P_, 0, lat0:lat0 + n],
                                           op=ALU.mult), [ps.t(), tab.t()], [xc.t()])
        self.dve(I("tensor_tensor", out=xs.h[0:P_, 0:n], in0=ps.h[0:P_, 0:n], in1=tab.h[0:P_, 1, lat0:lat0 + n],
                                           op=ALU.mult), [ps.t(), tab.t()], [xs.t()])

        def stage2():
            self.mm(pr.h[0:P_, 0:n], pr.t(), [(self.IDB.h[0:P_, 0:P_], xc.h[0:P_, 0:n]),
                                               (self.RTB.h[0:P_, 0:P_], xs.h[0:P_, 0:n])],
                    [self.IDB.t(), self.RTB.t(), xc.t(), xs.t()])
            self.act(I("copy", out=out_ap, in_=pr.h[0:P_, 0:n]), [pr.t()], [out_t])
        return stage2

    def transpose_to_tok(self, src, src_t, tok0, nchunk, dst, dst_tile_fn, pbank, c0=0):
        done = 0
        while done < nchunk:
            g = min(8, nchunk - done)
            pv = pbank.h[:].bitcast(BF16)
            for j in range(g):
                t0 = tok0 + (done + j) * 128
                self.pe(I("transpose", pv[:, j * 128:(j + 1) * 128], src.h[:, t0:t0 + 128],
                                                                 self.IDB.h[:]),
                        [src_t, self.IDB.t()], [pbank.t()])
            self.dve(I("tensor_copy",
                out=dst.h[:, c0 + done:c0 + done + g, :], in_=pv[:, 0:g * 128].rearrange("p (c f) -> p c f", f=128)),
                [pbank.t()], [dst_tile_fn(done)])
            done += g

    def layer(self, li, l, qblocks, ho, xs_cols=None, kv="compute"):
        S = self.S
        P = self.P
        nl = len(self.layers)
        upd_ctx = (qblocks[0][2] == 1)
        lam_init = 0.8 - 0.6 * math.exp(-0.3 * l)
        kblocks = [(0, 256, 1), (256, 512, 0), (768, 512, 0), (1280, 512, 0), (1792, 512, 0)]
        H = self.H

        S.dma("sp", [(self.BC.h[:], self.bc_d[li])], writes=[self.BC.t()], semkey="bc")
        S.dma("pool", [(self.WSPT.h[:], self.wspt_d[li])], writes=[self.WSPT.t()], semkey="wsp")
        lamp = self.BC.h[:, 1024:1280].rearrange("p (a d) -> p a d", d=64)
        self.dve(I("tensor_tensor", out=self.LTMP.h[:, 0, :], in0=lamp[:, 0, :], in1=lamp[:, 1, :], op=ALU.mult),
                 [self.BC.t()], [self.LTMP.t()])
        self.dve(I("tensor_tensor", out=self.LTMP.h[:, 1, :], in0=lamp[:, 2, :], in1=lamp[:, 3, :], op=ALU.mult),
                 [self.BC.t()], [self.LTMP.t()])
        self.dve(I("tensor_reduce", out=self.LAMV.h[:, 2:4], in_=self.LTMP.h[:], axis=AX.X, op=ALU.add),
                 [self.LTMP.t()], [self.LAMV.t()])
        self.act(I("activation", out=self.LAMV.h[:, 4:6], in_=self.LAMV.h[:, 2:4], func=AF.Exp),
                 [self.LAMV.t()], [self.LAMV.t()])
        self.dve(I("scalar_tensor_tensor", out=self.LAMV.h[:, 0:1], in0=self.LAMV.h[:, 5:6], scalar=-lam_init,
                                                  in1=self.LAMV.h[:, 4:5], op0=ALU.add, op1=ALU.subtract),
                 [self.LAMV.t()], [self.LAMV.t()])
        gso = self.sm_off(li, "gsub")
        self.dve(I("tensor_scalar", out=self.LAMV.h[:, 1:2], in0=self.SM.h[:, gso:gso + 1], scalar1=1.0 - lam_init,
                                           scalar2=0.0, op0=ALU.mult, op1=ALU.add), [self.SM.t()], [self.LAMV.t()])

        GS = self.GS[li]

        def n1_a1(seg):
            t0 = seg * 256
            xb = self.XSEG[seg % 2]
            if li == 0:
                src = self.xin[:, t0:t0 + 256].rearrange("(k p) t -> p k t", p=128)
                S.dma("sp", [(xb.h[:], src)], writes=[xb.t()], semkey=f"xseg{seg % 2}")
            else:
                src = self.XSD.h[:, t0:t0 + 256].rearrange("(k p) t -> p k t", p=128)
                S.dma("sp", [(xb.h[:], src)], reads=[self.XSD.t(0 if seg < 5 else 1)], writes=[xb.t()],
                      semkey=f"xseg{seg % 2}")
            SQ = self.SQb[seg % 2]
            self.act(I("activation", out=SQ.h[:], in_=xb.h[:], func=AF.Square), [xb.t()], [SQ.t()])
            ps = P[seg % 2]
            self.mm(ps.h[:, 0:256], ps.t(), [(self.ones, SQ.h[:, k, :]) for k in range(16)], [self.MATS.t(), SQ.t()])

        def n1_a2(seg):
            xb = self.XSEG[seg % 2]
            SQ = self.SQb[seg % 2]
            RS = self.RSb[seg % 2]
            self.rstd_from(P[seg % 2], 256, RS, 1.0 / D)
            self.dve(I("tensor_tensor", out=SQ.h[:], in0=xb.h[:],
                       in1=RS.h[:, 0:256].unsqueeze(1).broadcast_to([128, 16, 256]), op=ALU.mult),
                     [xb.t(), RS.t()], [SQ.t()])

        def n1_b(seg):
            t0 = seg * 256
            c = 1 if seg == 0 else 0
            SQ = self.SQb[seg % 2]
            for k in range(16):
                self.act(I("activation", out=H.h[:, k, t0:t0 + 256], in_=SQ.h[:, k, :], func=AF.Identity,
                           bias=self.modc(li, 0, k, c), scale=GS.h[:, 0, k, c:c + 1]),
                         [SQ.t(), self.MODV[li].t(0), GS.t(0)], [H.t(seg)])
        segs = list(range(9)) if kv != "load" else list(range((qblocks[0][0] + ho) // 256, 9))
        n1_a1(segs[0])
        n1_a2(segs[0])
        for si, seg in enumerate(segs):
            if si + 1 < len(segs):
                n1_a1(segs[si + 1])
            n1_b(seg)
            if si + 1 < len(segs):
                n1_a2(segs[si + 1])
        self.dump("h", H, H.h[:, :, 0:NQ], [H.t(s) for s in range(5)], [128, 16, NQ], BF16)

        def hts(t0, n):
            return [H.t(s) for s in range(t0 // 256, (t0 + n + 255) // 256)]

        S.dma("sp", [(self.TAB.h[:], self.tab_d)], writes=[self.TAB.t()], semkey="tab")

        HD = self.HD
        pj_i = [0]

        def pj():
            b = P[pj_i[0] % 4]
            pj_i[0] += 1
            return b

        def proj(w, wt, m0, M, t0, n, src=None, src_tiles=None, nk=16, hoff=0):
            ps = pj()
            if src is None:
                src = H
                t0 = t0 + hoff
                src_tiles = hts(t0, n)
            self.mm(ps.h[0:M, 0:n], ps.t(), [(w[:, k, m0:m0 + M], src.h[:, k, t0:t0 + n]) for k in range(nk)],
                    [wt] + src_tiles)
            return ps

        U, GVT = self.U, self.GVT
        for (t0, n, c) in qblocks:
            wgu = [self.wload("C", self.wsrc(self.w_in, li, 0, 16, C_GU + 256 * i, 256), 16, 256) for i in range(2)]
            for g in range(4):
                w, wt = wgu[g // 2]
                ps = proj(w, wt, (g % 2) * 128, 128, t0, n, hoff=ho)
                self.act(I("activation", out=U.h[:, g, 0:n], in_=ps.h[:, 0:n], func=AF.Gelu),
                         [ps.t()], [U.t(g)])
            wgv = [self.wload("C", self.wsrc(self.w_in, li, 0, 16, C_GV + 256 * i, 256), 16, 256) for i in range(2)]
            for g in range(4):
                w, wt = wgv[g // 2]
                ps = proj(w, wt, (g % 2) * 128, 128, t0, n, hoff=ho)
                self.act(I("activation", out=GVT.h[:, g, 0:n], in_=ps.h[:, 0:n], func=AF.Gelu),
                         [ps.t()], [GVT.t(g)])
            nch = n // 128
            pendg = []

            def flushg():
                for f in pendg:
                    f()
                del pendg[:]
            for m in range(nch):
                pt = P[4 + (m % 2)]
                for g in range(4):
                    self.pe(I("transpose", pt.h[:, g * 128:(g + 1) * 128],
                                                                    GVT.h[:, g, m * 128:(m + 1) * 128], self.ident),
                            [GVT.t(g), self.MATS.t()], [pt.t()])
                flushg()
                vf = self.VF[m % 2]
                self.act(I("copy", out=vf.h[:], in_=pt.h[:]), [pt.t()], [vf.t()])
                mx = self.MIXT
                self.dve(I("tensor_tensor", out=mx.h[:], in0=vf.h[:], in1=vf.h[:], op=ALU.mult),
                         [vf.t()], [mx.t()])
                gst = self.GST
                self.dve(I("tensor_reduce", out=gst.h[:, 0:4], in_=mx.h[:].rearrange("p (g c) -> p g c", g=4),
                                                   axis=AX.X, op=ALU.add), [mx.t()], [gst.t()])
                self.act(I("activation", out=gst.h[:, 4:8], in_=gst.h[:, 0:4], func=AF.Sqrt, bias=EPS,
                                                scale=1.0 / 128), [gst.t()], [gst.t()])
                self.dve(I("reciprocal", out=gst.h[:, 8:12], in_=gst.h[:, 4:8]), [gst.t()], [gst.t()])
                self.dve(I("tensor_tensor", out=vf.h[:].rearrange("p (g c) -> p g c", g=4),
                                                         in0=vf.h[:].rearrange("p (g c) -> p g c", g=4),
                                                         in1=gst.h[:, 8:12].unsqueeze(2).broadcast_to([128, 4, 128]),
                                                         op=ALU.mult), [vf.t(), gst.t()], [vf.t()])
                vt = self.VTOK[m % 2]
                self.dve(I("tensor_tensor", out=vt.h[:], in0=vf.h[:], in1=self.BC.h[:, 0:512],
                                                                op=ALU.mult), [vf.t(), self.BC.t()], [vt.t()])
                def mix(m=m, vt=vt, tt=t0 + m * 128):
                    pm = P[6]
                    mx2 = self.MIXT2
                    for g in range(4):
                        self.mm(pm.h[:, g * 128:(g + 1) * 128], pm.t(),
                                [(vt.h[:, g * 128:(g + 1) * 128], self.WSPT.h[:, g, :])], [vt.t(), self.WSPT.t()])
                    self.dve(I("tensor_tensor", out=mx2.h[:], in0=pm.h[:], in1=self.BC.h[:, 512:1024], op=ALU.add),
                             [pm.t(), self.BC.t()], [mx2.t()])
                    self.dve(I("tensor_tensor",
                               out=HD.h[:, 6:10, tt:tt + 128], in0=mx2.h[:].rearrange("p (g c) -> p g c", g=4),
                               in1=U.h[:, :, m * 128:(m + 1) * 128], op=ALU.mult),
                             [mx2.t()] + [U.t(g) for g in range(4)], [HD.t(("g", tt))])
                pendg.append(mix)
            flushg()
        self.dump("gated", HD, HD.h[:, 6:10, :], [HD.t(("g", tt)) for tt in range(0 if upd_ctx else 256, NQ, 128)],
                  [128, 4, NQ], BF16)

        QT, KT, VT, V = self.QT, self.KT, self.VT, self.V
        pend_epi = []
        pend_epb = []
        for h in range(6):
            wq = self.wload("C", self.wsrc(self.w_in, li, 0, 16, C_Q + 128 * h, 128), 16, 128)
            ri = 0
            pend = []

            def flush():
                for f in pend:
                    f()
                del pend[:]
            if kv == "load":
                S.dma("sp", [(KT.h[:], self.KVS.h[h, 0])], reads=[self.KVS.t(h)],
                      writes=[KT.t(t0) for (t0, n, c) in kblocks], semkey="kvl0")
                S.dma("sp", [(V.h[:], self.KVS.h[h, 1].rearrange("p (c f) -> p c f", f=128))], reads=[self.KVS.t(h)],
                      writes=[V.t(t0 // 128) for (t0, n, c) in kblocks], semkey="kvl1")
            else:
                wk = self.wload("C", self.wsrc(self.w_in, li, 0, 16, C_K + 128 * h, 128), 16, 128)
                wv = self.wload("C", self.wsrc(self.w_in, li, 0, 16, C_V + 128 * h, 128), 16, 128)
                for (t0, n, c) in kblocks:
                    ps = proj(wv[0], wv[1], 0, 128, t0, n)
                    self.act(I("copy", out=VT.h[:, t0:t0 + n], in_=ps.h[:, 0:n]), [ps.t()], [VT.t(t0)])
                for (t0, n, c) in kblocks:
                    ps = proj(wk[0], wk[1], 0, 128, t0, n)
                    flush()
                    if c == 1:
                        self.act(I("copy", out=KT.h[:, t0:t0 + n], in_=ps.h[:, 0:n]), [ps.t()], [KT.t(t0)])
                    else:
                        pend.append(self.rope_evac(ps, n, 128, t0 - 256, KT.h[:, t0:t0 + n], KT.t(t0), self.XC[ri % 2],
                                                   self.XS_[ri % 2], P[4 + ri % 2]))
                        ri += 1
                for (t0, n, c) in kblocks:
                    self.transpose_to_tok(VT, VT.t(t0), t0, n // 128, V, lambda d, t0=t0: V.t(t0 // 128), P[6], c0=t0 // 128)
            for (t0, n, c) in qblocks:
                ps = proj(wq[0], wq[1], 0, 128, t0, n, hoff=ho)
                flush()
                if c == 1:
                    self.act(I("copy", out=QT.h[:, t0:t0 + n], in_=ps.h[:, 0:n]), [ps.t()], [QT.t(t0)])
                else:
                    pend.append(self.rope_evac(ps, n, 128, t0 + ho - 256, QT.h[:, t0:t0 + n], QT.t(t0), self.XC[ri % 2],
                                               self.XS_[ri % 2], P[4 + ri % 2]))
                    ri += 1
            flush()
            kt_all = [KT.t(t0) for (t0, n, c) in kblocks]
            v_all = [V.t(t0 // 128) for (t0, n, c) in kblocks]
            if kv == "save":
                S.dma("sp", [(self.KVS.h[h, 0], KT.h[:]), (self.KVS.h[h, 1].rearrange("p (c f) -> p c f", f=128), V.h[:])],
                      reads=kt_all + v_all, writes=[self.KVS.t(h)], semkey="kvs")
            for (q0, nq, c) in qblocks:
                kcs = list(range(2)) if c == 1 else list(range(18))
                o1, o2, s1, s2 = P[4], P[5], P[6], P[7]
                nk = len(kcs)

                def s_mm(i):
                    kc = kcs[i]
                    pa, pb = P[(2 * i) % 4], P[(2 * i + 1) % 4]
                    self.mm(pa.h[:, 0:nq], pa.t(), [(KT.h[0:64, kc * 128:(kc + 1) * 128], QT.h[0:64, q0:q0 + nq])],
                            kt_all + [QT.t(q0)])
                    self.mm(pb.h[:, 0:nq], pb.t(), [(KT.h[64:128, kc * 128:(kc + 1) * 128], QT.h[64:128, q0:q0 + nq])],
                            kt_all + [QT.t(q0)])
                    return pa, pb
                pend = s_mm(0)
                for i in range(nk):
                    pa, pb = pend
                    ea, eb = self.ED[(2 * i) % 4], self.ED[(2 * i + 1) % 4]
                    self.act(I("activation", out=ea.h[:, 0:nq], in_=pa.h[:, 0:nq], func=AF.Exp, scale=DA_SCALE),
                             [pa.t()], [ea.t()])
                    self.act(I("activation", out=eb.h[:, 0:nq], in_=pb.h[:, 0:nq], func=AF.Exp, scale=DA_SCALE),
                             [pb.t()], [eb.t()])
                    if i + 1 < nk:
                        pend = s_mm(i + 1)
                    kc = kcs[i]
                    st, sp_ = (i == 0), (i == nk - 1)
                    for (ob, sb_, eb_) in ((o1, s1, ea), (o2, s2, eb)):
                        self.pe(I("matmul",
                            ob.h[:, 0:nq], lhsT=V.h[:, kc, :], rhs=eb_.h[:, 0:nq], start=st, stop=sp_),
                            v_all + [eb_.t()], [ob.t()])
                        self.pe(I("matmul",
                            sb_.h[:, 0:nq], lhsT=self.ONB.h[:], rhs=eb_.h[:, 0:nq], start=st, stop=sp_),
                            [self.ONB.t(), eb_.t()], [sb_.t()])
                    if i == min(1, nk - 1):
                        for f in pend_epb:
                            f()
                        del pend_epb[:]
                for f in pend_epi:
                    f()
                del pend_epi[:]
                T1, T2, T3, T4, T5 = self.EPD
                self.dve(I("reciprocal", out=T1.h[:, 0:nq], in_=s1.h[:, 0:nq]), [s1.t()], [T1.t()])
                self.act(I("copy", out=T4.h[:, 0:nq], in_=o1.h[:, 0:nq]), [o1.t()], [T4.t()])
                self.dve(I("reciprocal", out=T2.h[:, 0:nq], in_=s2.h[:, 0:nq]), [s2.t()], [T2.t()])
                self.act(I("copy", out=T5.h[:, 0:nq], in_=o2.h[:, 0:nq]), [o2.t()], [T5.t()])

                def epb(nq=nq, T1=T1, T2=T2, T4=T4, T5=T5):
                    self.dve(I("tensor_tensor", out=T1.h[:, 0:nq], in0=T4.h[:, 0:nq], in1=T1.h[:, 0:nq], op=ALU.mult),
                             [T4.t(), T1.t()], [T1.t()])
                    self.dve(I("tensor_tensor", out=T2.h[:, 0:nq], in0=T5.h[:, 0:nq], in1=T2.h[:, 0:nq], op=ALU.mult),
                             [T5.t(), T2.t()], [T2.t()])
                    self.dve(I("scalar_tensor_tensor", out=T1.h[:, 0:nq], in0=T2.h[:, 0:nq], scalar=self.LAMV.h[:, 0:1],
                               in1=T1.h[:, 0:nq], op0=ALU.mult, op1=ALU.add),
                             [T1.t(), T2.t(), self.LAMV.t()], [T1.t()])
                    self.act(I("activation", out=T4.h[:, 0:nq], in_=T1.h[:, 0:nq], func=AF.Square), [T1.t()], [T4.t()])
                pend_epb.append(epb)

                def epi2(h=h, q0=q0, nq=nq, T1=T1, T2=T4, T3=T3):
                    pst = P[0]
                    self.mm(pst.h[:, 0:nq], pst.t(), [(self.ones, T2.h[:, 0:nq])], [self.MATS.t(), T2.t()])
                    self.rstd_from(pst, nq, T3, 1.0 / 128)
                    self.dve(I("scalar_tensor_tensor",
                               out=HD.h[:, h, q0:q0 + nq], in0=T1.h[:, 0:nq], scalar=self.LAMV.h[:, 1:2], in1=T3.h[:, 0:nq],
                               op0=ALU.mult, op1=ALU.mult), [T1.t(), T3.t(), self.LAMV.t()], [HD.t((h, q0))])
                pend_epi.append(epi2)
        for f in pend_epb + pend_epi:
            f()
        del pend_epb[:]
        del pend_epi[:]
        self.dump("oda", HD, HD.h[:, 0:6, :], [HD.t((h, q0)) for h in range(6) for (q0, _, _) in qblocks], [128, 6, NQ], BF16)

        CQN, CKVN, KRT, CQF = self.CQN, self.CKVN, self.KRT, self.CQF
        for which in (("cq",) if kv == "load" else ("cq", "ckv")):
            col0 = C_CQ if which == "cq" else C_CKV
            blocks = qblocks if which == "cq" else kblocks
            dst = CQN if which == "cq" else CKVN
            go = self.sm_off(li, "gq" if which == "cq" else "gkv")
            for (t0, n, c) in blocks:
                ws = [self.wload("C", self.wsrc(self.w_in, li, 0, 16, col0 + 256 * i, 256), 16, 256) for i in range(2)]
                pstat = P[4]
                for j in range(4):
                    w, wt = ws[j // 2]
                    ps = proj(w, wt, (j % 2) * 128, 128, t0, n, hoff=(ho if which == "cq" else 0))
                    self.act(I("copy", out=CQF.h[:, j, 0:n], in_=ps.h[:, 0:n]), [ps.t()], [CQF.t(j)])
                    sq = self.SQT[j % 2]
                    self.act(I("activation", out=sq.h[:, 0:n], in_=ps.h[:, 0:n], func=AF.Square),
                             [ps.t()], [sq.t()])
                    self.pe(I("matmul", pstat.h[:, 0:n], lhsT=self.ones, rhs=sq.h[:, 0:n],
                                                             start=(j == 0), stop=(j == 3)),
                            [self.MATS.t(), sq.t()], [pstat.t()])
                self.rstd_from(pstat, n, self.RSM, 1.0 / 512)
                for j in range(4):
                    self.dve(I("scalar_tensor_tensor",
                        out=dst.h[:, j, t0:t0 + n], in0=CQF.h[:, j, 0:n], scalar=self.SM.h[:, go + j:go + j + 1],
                        in1=self.RSM.h[:, 0:n], op0=ALU.mult, op1=ALU.mult),
                        [CQF.t(j), self.RSM.t(), self.SM.t()], [dst.t(t0)])
        if kv == "load":
            S.dma("sp", [(CKVN.h[:], self.MKS.h[0:4].rearrange("c p t -> p c t"))], reads=[self.MKS.t()],
                  writes=[CKVN.t(t0) for (t0, n, c) in kblocks], semkey="mkl0")
            S.dma("sp", [(KRT.h[0:64, :], self.MKS.h[4, 0:64, :])], reads=[self.MKS.t()],
                  writes=[KRT.t(t0) for (t0, n, c) in kblocks], semkey="mkl1")
        else:
            wkr = self.wload("C", self.wsrc(self.w_in, li, 0, 16, C_KR, 64), 16, 64)
            for (t0, n, c) in kblocks:
                ps = proj(wkr[0], wkr[1], 0, 64, t0, n)
                if c == 1:
                    self.act(I("copy", out=KRT.h[0:64, t0:t0 + n], in_=ps.h[0:64, 0:n]), [ps.t()], [KRT.t(t0)])
                else:
                    self.rope_evac(ps, n, 64, t0 - 256, KRT.h[0:64, t0:t0 + n], KRT.t(t0), self.XC2, self.XS2, P[5])()
            if kv == "save":
                S.dma("sp", [(self.MKS.h[0:4].rearrange("c p t -> p c t"), CKVN.h[:]), (self.MKS.h[4, 0:64, :], KRT.h[0:64, :])],
                      reads=[CKVN.t(t0) for (t0, n, c) in kblocks] + [KRT.t(t0) for (t0, n, c) in kblocks],
                      writes=[self.MKS.t()], semkey="mks")
        self.dump("cqn", CQN, CQN.h[:], [CQN.t(t0) for (t0, _, _) in qblocks], [128, 4, NQ], BF16)

        QN, QR, KN, VMT, VM = self.QN, self.QR, self.KN, self.VMT, self.VM
        cqn_all = [CQN.t(t0) for (t0, _, _) in qblocks]
        ckvn_all = [CKVN.t(t0) for (t0, _, _) in kblocks]
        krt_all = [KRT.t(t0) for (t0, _, _) in kblocks]
        for h in range(6):
            wuq = self.wload("C", self.wsrc(self.w_uq, li, 0, 4, 192 * h, 192), 4, 192)
            wukv = self.wload("C", self.wsrc(self.w_ukv, li, 0, 4, 256 * h, 256), 4, 256)
            ri = 0
            pendm = []

            def flushm():
                for f in pendm:
                    f()
                del pendm[:]
            for (t0, n, c) in qblocks:
                ps = proj(wuq[0], wuq[1], 0, 128, t0, n, src=CQN, src_tiles=[CQN.t(t0)], nk=4)
                flushm()
                self.act(I("copy", out=QN.h[:, t0:t0 + n], in_=ps.h[:, 0:n]), [ps.t()], [QN.t(t0)])
                ps = proj(wuq[0], wuq[1], 128, 64, t0, n, src=CQN, src_tiles=[CQN.t(t0)], nk=4)
                if c == 1:
                    self.act(I("copy", out=QR.h[0:64, t0:t0 + n], in_=ps.h[0:64, 0:n]), [ps.t()], [QR.t(t0)])
                else:
                    pendm.append(self.rope_evac(ps, n, 64, t0 + ho - 256, QR.h[0:64, t0:t0 + n], QR.t(t0), self.XC2, self.XS2, P[5]))
            for (t0, n, c) in kblocks:
                ps = proj(wukv[0], wukv[1], 0, 128, t0, n, src=CKVN, src_tiles=[CKVN.t(t0)], nk=4)
                flushm()
                self.act(I("copy", out=KN.h[:, t0:t0 + n], in_=ps.h[:, 0:n]), [ps.t()], [KN.t(t0)])
                ps = proj(wukv[0], wukv[1], 128, 128, t0, n, src=CKVN, src_tiles=[CKVN.t(t0)], nk=4)
                self.act(I("copy", out=VMT.h[:, t0:t0 + n], in_=ps.h[:, 0:n]), [ps.t()], [VMT.t(t0)])
            for (t0, n, c) in kblocks:
                self.transpose_to_tok(VMT, VMT.t(t0), t0, n // 128, VM, lambda d, t0=t0: VM.t(t0 // 128), P[6], c0=t0 // 128)
            kn_all = [KN.t(t0) for (t0, n, c) in kblocks]
            vm_all = [VM.t(t0 // 128) for (t0, n, c) in kblocks]
            for (q0, nq, c) in qblocks:
                kcs = list(range(2)) if c == 1 else list(range(18))
                ob, sb_ = P[4], P[7]
                nk = len(kcs)

                def s_mm(i):
                    kc = kcs[i]
                    pa = P[i % 4]
                    self.mm(pa.h[:, 0:nq], pa.t(),
                            [(KN.h[:, kc * 128:(kc + 1) * 128], QN.h[:, q0:q0 + nq]),
                             (KRT.h[0:64, kc * 128:(kc + 1) * 128], QR.h[0:64, q0:q0 + nq])],
                            kn_all + krt_all + [QN.t(q0), QR.t(q0)])
                    return pa
                pend = s_mm(0)
                for i in range(nk):
                    pa = pend
                    ea = self.EM[i % 4]
                    self.act(I("activation", out=ea.h[:, 0:nq], in_=pa.h[:, 0:nq], func=AF.Exp, scale=MLA_SCALE),
                             [pa.t()], [ea.t()])
                    if i + 1 < nk:
                        pend = s_mm(i + 1)
                    kc = kcs[i]
                    st, sp_ = (i == 0), (i == nk - 1)
                    self.pe(I("matmul",
                        ob.h[:, 0:nq], lhsT=VM.h[:, kc, :], rhs=ea.h[:, 0:nq], start=st, stop=sp_),
                        vm_all + [ea.t()], [ob.t()])
                    self.pe(I("matmul",
                        sb_.h[:, 0:nq], lhsT=self.ONB.h[:], rhs=ea.h[:, 0:nq], start=st, stop=sp_),
                        [self.ONB.t(), ea.t()], [sb_.t()])
                    if i % 6 == 5:
                        self.mod_ring, self.mod_bank = "C", 6
                        self.pump(li, 1)
                        self.mod_ring, self.mod_bank = "B", 7
                T1, T2 = self.EPM
                self.dve(I("reciprocal", out=T1.h[:, 0:nq], in_=sb_.h[:, 0:nq]), [sb_.t()], [T1.t()])
                self.act(I("copy", out=T2.h[:, 0:nq], in_=ob.h[:, 0:nq]), [ob.t()], [T2.t()])
                self.dve(I("tensor_tensor", out=HD.h[:, 10 + h, q0:q0 + nq], in0=T2.h[:, 0:nq],
                                                                       in1=T1.h[:, 0:nq], op=ALU.mult),
                         [T2.t(), T1.t()], [HD.t((10 + h, q0))])
        self.mod_ring, self.mod_bank = "C", 6
        self.pump(li, 48)
        self.mod_ring, self.mod_bank = "B", 7
        self.dump("heads", HD, HD.h[:], [HD.t((10 + h, q0)) for h in range(6) for (q0, _, _) in qblocks], [128, 16, NQ], BF16)

        X = self.X
        hd_all = list(HD.tiles.values())
        tq0 = qblocks[0][0]
        for jb in range(8):
            w, wt = self.wload("A", self.wsrc(self.w_out, li, 0, 16, 256 * jb, 256), 16, 256)
            for jj in range(2):
                j = 2 * jb + jj
                xo = self.XOLD[j % 2]
                if li == 0:
                    src = self.xin[j * 128:(j + 1) * 128, tq0 + ho:NQ + ho]
                else:
                    src = self.XSD.h[j * 128:(j + 1) * 128, tq0 + ho:NQ + ho]
                S.dma("sp", [(xo.h[:, tq0:NQ], src)], reads=([self.XSD.t(0)] if li else []), writes=[xo.t()], semkey=f"xold{j % 2}")
                for bi, (t0, n, c) in enumerate(qblocks):
                    ps = P[(j * 3 + bi) % 6]
                    self.mm(ps.h[:, 0:n], ps.t(), [(w[:, k, jj * 128:(jj + 1) * 128], HD.h[:, k, t0:t0 + n]) for k in range(16)],
                            [wt] + hd_all)
                    self.dve(I("scalar_tensor_tensor",
                        out=X.h[:, j, t0:t0 + n], in0=ps.h[:, 0:n], scalar=self.modc(li, 2, j, c), in1=xo.h[:, t0:t0 + n],
                        op0=ALU.mult, op1=ALU.add), [ps.t(), xo.t(), self.MODV[li].t(2)], [X.t((j, t0))])
        self.dump("xmix", X, X.h[:], list(X.tiles.values()), [128, 16, NQ])

        H2, AJ = self.H2, self.AJ
        GS = self.GS[li]
        for bi, (t0, n, c) in enumerate(qblocks):
            pst = P[6]
            for k in range(16):
                sq = self.SQ2[k % 3]
                self.act(I("activation", out=sq.h[:, 0:n], in_=X.h[:, k, t0:t0 + n], func=AF.Square),
                         [X.t((k, t0))], [sq.t()])
                self.pe(I("matmul", pst.h[:, 0:n], lhsT=self.ones, rhs=sq.h[:, 0:n], start=(k == 0), stop=(k == 15)),
                        [self.MATS.t(), sq.t()], [pst.t()])
            self.rstd_from(pst, n, self.RS2, 1.0 / D)
            for k in range(16):
                tmp = self.N2T[k % 2]
                self.dve(I("scalar_tensor_tensor",
                    out=tmp.h[:, 0:n], in0=X.h[:, k, t0:t0 + n], scalar=GS.h[:, 1, k, c:c + 1], in1=self.RS2.h[:, 0:n],
                    op0=ALU.mult, op1=ALU.mult), [X.t((k, t0)), GS.t(1), self.RS2.t()], [tmp.t()])
                self.act(I("activation",
                    out=H2.h[:, k, t0:t0 + n], in_=tmp.h[:, 0:n], func=AF.Identity, bias=self.modc(li, 3, k, c), scale=1.0),
                    [tmp.t(), self.MODV[li].t(3)], [H2.t(t0)])
        h2_all = [H2.t(t0) for (t0, _, _) in qblocks]
        self.dump("h2", H2, H2.h[:], h2_all, [128, 16, NQ], BF16)
        nxt = li + 1 if li + 1 < nl else None
        f1i = 0
        for jb in range(8):
            for sub in range(4):
                w, wt = self.wload("B", self.wsrc(self.w_fc1, li, 0, 16, jb * 1024 + sub * 256, 256), 16, 256)
                for jj in range(2):
                    hc = sub * 2 + jj
                    for (t0, n, c) in qblocks:
                        ps = P[f1i % 3]
                        f1i += 1
                        self.mm(ps.h[:, 0:n], ps.t(), [(w[:, k, jj * 128:(jj + 1) * 128], H2.h[:, k, t0:t0 + n]) for k in range(16)],
                                [wt, H2.t(t0)])
                        tr = self.TMPR[f1i % 2]
                        self.act(I("activation", out=tr.h[:, 0:n], in_=ps.h[:, 0:n], func=AF.Relu),
                                 [ps.t()], [tr.t()])
                        self.dve(I("tensor_tensor", out=AJ.h[:, hc, t0:t0 + n], in0=tr.h[:, 0:n],
                                                                                    in1=tr.h[:, 0:n], op=ALU.mult),
                                 [tr.t()], [AJ.t((hc, t0))])
                if nxt is not None:
                    self.pump(nxt, 1)
            aj_all = list(AJ.tiles.values())
            for ob in range(8):
                w, wt = self.wload("B", self.wsrc(self.w_fc2, li, jb * 1024, 8, ob * 256, 256), 8, 256)
                for jj in range(2):
                    j = ob * 2 + jj
                    for (t0, n, c) in qblocks:
                        ps = P[3 + f1i % 3]
                        f1i += 1
                        self.mm(ps.h[:, 0:n], ps.t(), [(w[:, k, jj * 128:(jj + 1) * 128], AJ.h[:, k, t0:t0 + n]) for k in range(8)],
                                [wt] + aj_all)
                        self.dve(I("scalar_tensor_tensor",
                            out=X.h[:, j, t0:t0 + n], in0=ps.h[:, 0:n], scalar=self.modc(li, 5, j, c), in1=X.h[:, j, t0:t0 + n],
                            op0=ALU.mult, op1=ALU.add), [ps.t(), X.t((j, t0)), self.MODV[li].t(5)], [X.t((j, t0))])
                if nxt is not None and ob % 4 == 3:
                    self.pump(nxt, 1)
        if nxt is not None:
            self.pump(nxt, 48)
        x_all = list(X.tiles.values())
        self.dump("xout", X, X.h[:], x_all, [128, 16, NQ])

        if l == 1:
            gfo = 32 + nl * 137
            for bi, (t0, n, c) in enumerate(qblocks):
                pst = P[bi % 2]
                for k in range(16):
                    sq = self.SQ2[k % 3]
                    self.act(I("activation", out=sq.h[:, 0:n], in_=X.h[:, k, t0:t0 + n], func=AF.Square),
                             [X.t((k, t0))], [sq.t()])
                    self.pe(I("matmul", pst.h[:, 0:n], lhsT=self.ones, rhs=sq.h[:, 0:n],
                                                                      start=(k == 0), stop=(k == 15)),
                            [self.MATS.t(), sq.t()], [pst.t()])
                self.rstd_from(pst, n, self.RS2, 1.0 / D)
                for k in range(16):
                    ost = self.OST[k % 2]
                    self.dve(I("scalar_tensor_tensor",
                        out=ost.h[:, 0:n], in0=X.h[:, k, t0:t0 + n], scalar=self.SM.h[:, gfo + k:gfo + k + 1], in1=self.RS2.h[:, 0:n],
                        op0=ALU.mult, op1=ALU.mult), [X.t((k, t0)), self.SM.t(), self.RS2.t()], [ost.t()])
                    S.dma("sp", [(self.out_b.h[k * 128:(k + 1) * 128, t0 - 256:t0 - 256 + n], ost.h[:, 0:n])],
                          reads=[ost.t()], writes=[self.out_b.t((k, t0))], semkey=f"ost{k % 2}")
        else:
            lo, hi = xs_cols
            if nl == 1:
                S.dma("sp", [(self.out_b.h.rearrange("(k p) t -> p k t", p=128), X.h[:])], reads=x_all,
                      writes=[self.out_b.t()], semkey="xs_w")
            else:
                S.dma("sp", [(self.XSD.h[:, lo + ho:hi + ho].rearrange("(k p) t -> p k t", p=128), X.h[:, :, lo:hi])],
                      reads=x_all, writes=[self.XSD.t(0 if ho == 0 else 1)], semkey="xs_w")


def _rope_tables():
    rows = 2048 // 64
    row = np.repeat(np.arange(rows, dtype=np.float32), 64)
    col = np.tile(np.arange(64, dtype=np.float32), rows)
    n_f = 16
    inv = (np.float32(10000.0) ** (-np.arange(n_f, dtype=np.float32) / np.float32(n_f))).astype(np.float32)
    ang = np.concatenate([row[:, None] * inv, col[:, None] * inv], axis=-1).astype(np.float32)
    return np.cos(ang).astype(np.float32), np.sin(ang).astype(np.float32)


def _fm(v):
    return np.ascontiguousarray(v.reshape(-1, 128).T)


def _consts():
    ident = np.eye(128, dtype=np.float32)
    RT = np.zeros((128, 128), np.float32)
    for i in range(64):
        RT[2 * i + 1, 2 * i] = -1.0
        RT[2 * i, 2 * i + 1] = 1.0
    return np.concatenate([ident, RT, np.ones((128, 128), np.float32)], 1)


def _core_inputs(inp, layers, core, xin):
    b, s = core // 2, core % 2
    nl = len(layers)
    cos, sin = _rope_tables()
    order = np.concatenate([np.arange(s * 1024, (s + 1) * 1024), np.arange((1 - s) * 1024, (2 - s) * 1024)])
    pidx = (np.arange(128) % 64) // 2
    tab = np.stack([cos[order][:, pidx].T, sin[order][:, pidx].T], axis=1).astype(np.float32)
    sm = [np.stack([_fm(inp["c"][b]), _fm(inp["c_ctx"])], axis=2).reshape(128, 32)]
    for l in layers:
        sm += [_fm(inp["b_mod"][l]), _fm(inp["g_norm_mix"][l]), _fm(inp["g_norm_mlp"][l]), _fm(inp["g_mla_q"][l]),
               _fm(inp["g_mla_kv"][l]), _fm(inp["g_da_sub"][l])]
    sm.append(_fm(inp["g_final"]))
    smalls = np.ascontiguousarray(np.concatenate(sm, axis=1).astype(np.float32))
    wspt = np.stack([np.transpose(inp["w_spatial"][l], (2, 0, 1)) for l in layers]).astype(np.float32)
    bc = []
    for l in layers:
        row = np.concatenate([inp["g_gm_v"][l].reshape(512), inp["b_spatial"][l].reshape(512),
                              inp["lam_q1"][l], inp["lam_k1"][l], inp["lam_q2"][l], inp["lam_k2"][l]]).astype(np.float32)
        bc.append(np.broadcast_to(row[None, :], (128, 1280)))
    bc = np.ascontiguousarray(np.stack(bc))
    d = dict(xin=xin, smalls=smalls, mats=_consts(), ropetab=np.ascontiguousarray(tab),
             wspt=np.ascontiguousarray(wspt), bcast=bc)
    return d


def _weights(inp, layers):
    sl = slice(layers[0], layers[-1] + 1)
    return {k: np.ascontiguousarray(inp[k][sl]) for k in
            ("w_mod", "w_in", "w_mla_uq", "w_mla_ukv", "w_out", "w_fc1", "w_fc2")}


_PROG_CACHE = {}


def _get_prog(layers, fused, dbg=()):
    key = (tuple(layers), fused, tuple(dbg))
    if key not in _PROG_CACHE:
        p = Prog(list(layers), fused, dbg)
        p.build()
        _PROG_CACHE[key] = p
    return _PROG_CACHE[key]


def _xin_layer0(inp, core):
    b, s = core // 2, core % 2
    x = inp["x"][b]
    return np.ascontiguousarray(np.concatenate(
        [inp["ctx"][b].T, x[s * 1024:(s + 1) * 1024].T, x[(1 - s) * 1024:(2 - s) * 1024].T], axis=1).astype(np.float32))


def run_layers(inp, layers, xins, dbg=(), cores=range(8)):
    p = _get_prog(layers, False, dbg)
    w = _weights(inp, layers)
    in_maps = []
    for ci, core in enumerate(cores):
        d = _core_inputs(inp, layers, core, xins[ci])
        d.update(w)
        in_maps.append(d)
    res = run_bass_kernel_spmd(p.nc, in_maps, core_ids=list(range(len(in_maps))))
    return res.results


def kernel(**inp):
    inp = {k: np.asarray(v) for k, v in inp.items()}
    xin0 = [_xin_layer0(inp, c) for c in range(8)]
    res = run_layers(inp, [0, 1], xin0)
    out = np.empty((4, 2048, 2048), np.float32)
    for c in range(8):
        b, s = c // 2, c % 2
        out[b, s * 1024:(s + 1) * 1024, :] = res[c]["outT"].T
    return out
```

```python
import math
import numpy as np
import concourse.bass as bass
import concourse.mybir as mybir
from concourse.bass_utils import run_bass_kernel_spmd

F32 = mybir.dt.float32
BF16 = mybir.dt.bfloat16
AF = mybir.ActivationFunctionType
ALU = mybir.AluOpType
AX = mybir.AxisListType

D = 2048
KC = 16
NTOK = 2304
NQ = 1280
CTX = 256
OWN = 1024
DFF = 8192
EPS = 1e-6
DA_SCALE = 64 ** -0.5
MLA_SCALE = 192 ** -0.5
IN_COLS = 4416
C_Q, C_K, C_V, C_GU, C_GV, C_CQ, C_CKV, C_KR = 0, 768, 1536, 2304, 2816, 3328, 3840, 4352

ENGS = ("pe", "act", "dve", "pool", "sp")


def I(method, *a, **kw):
    return lambda e: getattr(e, method)(*a, **kw)


class T:
    __slots__ = ("name", "buf", "last_w", "readers")

    def __init__(self, name, buf):
        self.name = name
        self.buf = buf
        self.last_w = None
        self.readers = {}


class Buf:
    def __init__(self, S, name, space, start, nbytes, h=None):
        self.S = S
        self.name = name
        self.space = space
        self.start = start
        self.end = start + nbytes
        self.h = h
        self.tiles = {}
        self.overlaps = []
        self.stamp = -1
        self.acq = -1
        self.pending = []
        for o in S.bufs:
            if o.space == space and o.start < self.end and self.start < o.end:
                o.overlaps.append(self)
                self.overlaps.append(o)
        S.bufs.append(self)

    def t(self, key=0):
        tt = self.tiles.get(key)
        if tt is None:
            tt = T(f"{self.name}[{key}]", self)
            for i, p in enumerate(self.pending):
                tt.readers[("p", i)] = p
            self.tiles[key] = tt
        return tt

    def outstanding(self):
        out = []
        for tt in self.tiles.values():
            if tt.last_w is not None:
                out.append(tt.last_w)
            out.extend(tt.readers.values())
        return out


class Op:
    __slots__ = ("eng", "fn", "reads", "writes", "ndma", "semkey", "deps", "signal", "count", "name")

    def __init__(self, eng, fn, reads, writes, ndma, semkey, name):
        self.eng = eng
        self.fn = fn
        self.reads = reads
        self.writes = writes
        self.ndma = ndma
        self.semkey = semkey
        self.deps = []
        self.signal = False
        self.count = None
        self.name = name


class Sched:
    def __init__(self, nc):
        self.nc = nc
        self.ops = {e: [] for e in ENGS}
        self.all_ops = []
        self.bufs = []
        self.now = 0

    def _touch(self, B):
        if B.overlaps:
            for Y in B.overlaps:
                if Y.stamp > B.acq:
                    pend = Y.outstanding()
                    if pend:
                        base = len(B.pending)
                        B.pending = B.pending + pend
                        for tt in B.tiles.values():
                            for i, p in enumerate(pend):
                                tt.readers[("p", base + i)] = p
            B.acq = self.now
        B.stamp = self.now

    def op(self, eng, fn, reads=(), writes=(), ndma=0, semkey=None, name=""):
        self.now += 1
        o = Op(eng, fn, list(reads), list(writes), ndma, semkey, name)
        seen = set()
        for t in o.reads + o.writes:
            if id(t.buf) not in seen:
                seen.add(id(t.buf))
                self._touch(t.buf)
        deps = {}
        for t in o.reads:
            if t.last_w is not None:
                deps[id(t.last_w)] = t.last_w
        for t in o.writes:
            if t.last_w is not None:
                deps[id(t.last_w)] = t.last_w
            for r in t.readers.values():
                deps[id(r)] = r
        deps.pop(id(o), None)
        o.deps = list(deps.values())
        rkey = ("d", self.now) if ndma else eng
        for t in o.reads:
            t.readers[rkey] = o
        for t in o.writes:
            t.last_w = o
            t.readers = {}
        self.ops[eng].append(o)
        self.all_ops.append(o)
        return o

    def dma(self, eng, pairs, reads=(), writes=(), semkey=None, name=""):
        def fn(e, pairs=pairs):
            return [e.dma_start(out=o_, in_=i_) for (o_, i_) in pairs]
        return self.op(eng, fn, reads, writes, ndma=len(pairs), semkey=semkey, name=name)

    def final_wait(self, eng, tiles):
        return self.op(eng, lambda e: None, reads=tiles, name="final_wait")

    @staticmethod
    def _is_raw(o, d):
        ws = set(id(t) for t in d.writes)
        return any(id(t) in ws for t in o.reads)

    @staticmethod
    def _is_war(o, d):
        rs = set(id(t) for t in d.reads)
        return any(id(t) in rs for t in o.writes)

    def _needs_sem(self, o, d):
        if d.ndma:
            return True
        if d.eng == o.eng and not o.ndma:
            if d.eng == "pe":
                return False
            return self._is_raw(o, d) or self._is_war(o, d)
        return True

    def emit(self):
        nc = self.nc
        for o in self.all_ops:
            for d in o.deps:
                if not d.ndma and self._needs_sem(o, d):
                    d.signal = True
        semkeys = {}
        for e in ENGS:
            c = 0
            for o in self.ops[e]:
                if o.ndma:
                    semkeys[o.semkey] = semkeys.get(o.semkey, 0) + 16 * o.ndma
                    o.count = semkeys[o.semkey]
                elif o.signal:
                    c += 1
                    o.count = c
        sems = {}
        for e in ENGS:
            sems[("eng", e)] = nc.alloc_semaphore(name=f"s_{e}")
        for k in semkeys:
            sems[("dma", k)] = nc.alloc_semaphore(name=f"d_{k}")
        engobj = {"pe": nc.tensor, "act": nc.scalar, "dve": nc.vector, "pool": nc.gpsimd, "sp": nc.sync}
        self.nwaits = 0
        self.ninst = 0
        for e in ENGS:
            eng = engobj[e]
            known = {}
            for o in self.ops[e]:
                need = {}
                for d in o.deps:
                    if not self._needs_sem(o, d):
                        continue
                    key = ("dma", d.semkey) if d.ndma else ("eng", d.eng)
                    val = d.count
                    if known.get(key, 0) >= val:
                        continue
                    if need.get(key, 0) < val:
                        need[key] = val
                for key, val in need.items():
                    eng.wait_ge(sems[key], val)
                    known[key] = val
                    self.nwaits += 1
                r = o.fn(eng)
                if r is None:
                    continue
                self.ninst += 1
                if o.ndma:
                    for ins in r:
                        ins.then_inc(sems[("dma", o.semkey)], 16)
                elif o.signal:
                    r.then_inc(sems[("eng", e)], 1)


SB_BASE = 16512
SB_END = 229344
A0, B0, C0, D0 = SB_BASE, SB_BASE + 73728, SB_BASE + 114688, SB_BASE + 196608


class Prog:
    def __init__(self, layers, fused, dbg=()):
        self.layers = layers
        self.fused = fused
        self.dbg = dbg
        self.nc = bass.Bass("TRN2", target_bir_lowering=False)
        self.S = Sched(self.nc)
        self.dbg_outs = []
        self._n = 0

    def sb(self, name, shape, dtype, off):
        esz = 4 if dtype == F32 else 2
        nbytes = esz * int(np.prod(shape[1:]))
        assert off + nbytes <= SB_END, (name, off, nbytes)
        self._n += 1
        h = self.nc.alloc_sbuf_tensor_at(f"{name}_{self._n}", list(shape), dtype, offset=off)
        return Buf(self.S, name, "sb", off, nbytes, h)

    def dram_in(self, name, shape, dtype=F32):
        h = self.nc.dram_tensor(name, list(shape), dtype, kind="ExternalInput")
        return h.ap()

    def dram_out(self, name, shape, dtype=F32):
        h = self.nc.dram_tensor(name, list(shape), dtype, kind="ExternalOutput")
        b = Buf(self.S, name, "dram:" + name, 0, 1, h.ap())
        return b

    def dram_scratch(self, name, shape, dtype=F32):
        h = self.nc.dram_tensor(name, list(shape), dtype, kind="Internal")
        b = Buf(self.S, name, "dram:" + name, 0, 1, h.ap())
        return b

    def pe(self, fn, reads, writes):
        return self.S.op("pe", fn, reads, writes)

    def act(self, fn, reads, writes):
        return self.S.op("act", fn, reads, writes)

    def dve(self, fn, reads, writes):
        return self.S.op("dve", fn, reads, writes)

    def mm(self, ps_ap, ps_t, pairs, reads):
        n = len(pairs)
        for i, (l, r) in enumerate(pairs):
            self.pe(I("matmul", ps_ap, lhsT=l, rhs=r, start=(i == 0), stop=(i == n - 1)),
                    reads, [ps_t])

    def dump(self, name, buf, ap, tiles, shape, dtype=F32):
        if name not in self.dbg:
            return
        ob = self.dram_out("dbg_" + name, shape, dtype)
        self.S.dma("sp", [(ob.h, ap)], reads=tiles, writes=[ob.t()], semkey="dbg")
        self.dbg_outs.append(ob)

    def build(self):
        nc, S = self.nc, self.S
        L = self.layers
        nl = len(L)
        self.xin = self.dram_in("xin", [D, NTOK])
        self.smalls_d = self.dram_in("smalls", [128, self.n_smalls()])
        self.mats_d = self.dram_in("mats", [128, 384])
        self.tab_d = self.dram_in("ropetab", [128, 2, 2048])
        self.wspt_d = self.dram_in("wspt", [nl, 128, 4, 128])
        self.bc_d = self.dram_in("bcast", [nl, 128, 1280])
        self.w_mod = self.dram_in("w_mod", [nl, D, 6 * D])
        self.w_in = self.dram_in("w_in", [nl, D, IN_COLS])
        self.w_uq = self.dram_in("w_mla_uq", [nl, 512, 1152])
        self.w_ukv = self.dram_in("w_mla_ukv", [nl, 512, 1536])
        self.w_out = self.dram_in("w_out", [nl, D, D])
        self.w_fc1 = self.dram_in("w_fc1", [nl, D, DFF])
        self.w_fc2 = self.dram_in("w_fc2", [nl, DFF, D])
        last = (L[-1] == 1)
        if last:
            self.out_b = self.dram_out("outT", [D, OWN])
        else:
            self.out_b = self.dram_out("xs_out", [D, NQ])
        self.P = []
        for i in range(8):
            h = nc.alloc_psum_tensor(f"ps{i}", [128, 512], F32)
            self.P.append(Buf(S, f"P{i}", f"ps{i}", 0, 1, h))
        o = D0
        ns = self.n_smalls()
        self.SM = self.sb("smalls", [128, ns], F32, o); o += (4 * ns + 31) // 32 * 32
        self.MATS = self.sb("mats", [128, 384], F32, o); o += 1536
        self.IDB = self.sb("identb", [128, 128], BF16, o); o += 256
        self.ONB = self.sb("onesb", [128, 128], BF16, o); o += 256
        self.SIL = self.sb("sil", [128, 16, 2], BF16, o); o += 64
        self.MODV = [self.sb(f"modv{i}", [128, 96, 2], F32, o + 768 * i) for i in range(2)]; o += 1536
        self.GS = [self.sb(f"gs{i}", [128, 2, 16, 2], F32, o + 256 * i) for i in range(2)]; o += 512
        self.LAMV = self.sb("lamv", [128, 8], F32, o); o += 32
        self.WSPT = self.sb("wsptb", [128, 4, 128], BF16, o); o += 1024
        self.BC = self.sb("bc", [128, 1280], F32, o); o += 5120
        self.LTMP = self.sb("ltmp", [128, 2, 64], F32, o); o += 512
        self.RTB = self.sb("rtb", [128, 128], BF16, o); o += 256
        assert o <= SB_END, o
        self.H = self.sb("H", [128, 16, NTOK], BF16, A0)
        self.H2 = self.sb("H2", [128, 16, NQ], BF16, A0)
        self.AJ = self.sb("AJ", [128, 8, NQ], BF16, A0 + 40960)
        self.SQ2 = [self.sb(f"SQ2_{i}", [128, 512], F32, A0 + 61440 + 2048 * i) for i in range(3)]
        self.TMPR = [self.sb(f"TMPR{i}", [128, 512], F32, A0 + 67584 + 2048 * i) for i in range(2)]
        self.QN = self.sb("QN", [128, NQ], BF16, A0)
        self.QR = self.sb("QR", [128, NQ], BF16, A0 + 2560)
        self.KN = self.sb("KN", [128, NTOK], BF16, A0 + 5120)
        self.VMT = self.sb("VMT", [128, NTOK], BF16, A0 + 9728)
        self.VM = self.sb("VM", [128, 18, 128], BF16, A0 + 14336)
        self.EM = [self.sb(f"EM{i}", [128, 512], BF16, A0 + 18944 + 1024 * i) for i in range(4)]
        self.EPM = [self.sb(f"EPM{i}", [128, 512], F32, A0 + 23040 + 2048 * i) for i in range(2)]
        self.RA = [self.sb(f"RA{i}", [128, 4096], BF16, A0 + 28672 + 8192 * i) for i in range(3)]
        self.XOLD = [self.sb(f"XOLD{i}", [128, NQ], F32, A0 + 53248 + 5120 * i) for i in range(2)]
        self.HD = self.sb("HD", [128, 16, NQ], BF16, B0)
        self.RB = [self.sb(f"RB{i}", [128, 4096], BF16, B0 + 8192 * i) for i in range(3)]
        self.RS2 = self.sb("RS2", [128, 512], F32, B0 + 24576)
        self.N2T = [self.sb(f"N2T{i}", [128, 512], F32, B0 + 26624 + 2048 * i) for i in range(2)]
        self.OST = [self.sb(f"OST{i}", [128, 512], F32, B0 + 30720 + 2048 * i) for i in range(2)]
        self.CQF = self.sb("CQF", [128, 4, 512], F32, B0 + 25600)
        self.SQT = [self.sb(f"SQT{i}", [128, 512], F32, B0 + 33792 + 2048 * i) for i in range(2)]
        self.RSM = self.sb("RSM", [128, 512], F32, B0 + 37888)
        self.X = self.sb("X", [128, 16, NQ], F32, C0)
        self.XSEG = [self.sb(f"XSEG{i}", [128, 16, 256], F32, C0 + 16384 * i) for i in range(2)]
        self.XSEG.append(self.sb("XSEG2", [128, 16, 256], F32, B0 + 24576))
        self.SQb = [self.sb(f"SQ{i}", [128, 16, 256], F32, C0 + 32768 + 16384 * i) for i in range(2)]
        self.RSb = [self.sb(f"RS{i}", [128, 256], F32, C0 + 65536 + 1024 * i) for i in range(2)]
        self.RC = [self.sb(f"RC{i}", [128, 4096], BF16, C0 + 8192 * i) for i in range(3)]
        self.TAB = self.sb("TAB", [128, 2, 2048], F32, C0 + 24576)
        c1 = C0 + 40960
        self.QT = self.sb("QT", [128, NQ], BF16, c1)
        self.KT = self.sb("KT", [128, NTOK], BF16, c1 + 2560)
        self.VT = self.sb("VT", [128, NTOK], BF16, c1 + 7168)
        self.V = self.sb("V", [128, 18, 128], BF16, c1 + 11776)
        self.XC = [self.sb(f"XC{i}", [128, 512], BF16, C0 + 57344 + 4096 * i) for i in range(2)]
        self.XS_ = [self.sb(f"XS{i}", [128, 512], BF16, C0 + 57344 + 4096 * i + 2048) for i in range(2)]
        self.ED = [self.sb(f"ED{i}", [128, 512], BF16, C0 + 65536 + 1024 * i) for i in range(4)]
        self.EPD = [self.sb(f"EPD{i}", [128, 512], F32, C0 + 69632 + 2048 * i) for i in range(5)]
        self.U = self.sb("U", [128, 4, 512], F32, c1)
        self.GVT = self.sb("GVT", [128, 4, 512], F32, c1 + 8192)
        self.VTOK = [self.sb(f"VTOK{i}", [128, 512], BF16, c1 + 16384 + 1024 * i) for i in range(2)]
        self.VF = [self.sb(f"VF{i}", [128, 512], F32, c1 + 18432 + 2048 * i) for i in range(2)]
        self.GST = self.sb("GST", [128, 16], F32, c1 + 22528)
        self.MIXT = self.sb("MIXT", [128, 512], F32, c1 + 22592)
        self.MIXT2 = self.sb("MIXT2", [128, 512], F32, c1 + 24640)
        self.CQN = self.sb("CQN", [128, 4, NQ], BF16, c1)
        self.CKVN = self.sb("CKVN", [128, 4, NTOK], BF16, c1 + 10240)
        self.KRT = self.sb("KRT", [128, NTOK], BF16, c1 + 28672)
        self.XC2 = self.sb("XC2", [128, 512], BF16, c1 + 33280)
        self.XS2 = self.sb("XS2", [128, 512], BF16, c1 + 35328)
        assert c1 + 37376 <= C0 + 81920
        if nl == 2:
            self.XSD = self.dram_scratch("xs_scr", [D, NTOK])
            self.KVS = self.dram_scratch("kv_scr", [6, 2, 128, NTOK], BF16)
            self.MKS = self.dram_scratch("mk_scr", [5, 128, NTOK], BF16)

        S.dma("sp", [(self.SM.h[:], self.smalls_d)], writes=[self.SM.t()], semkey="c0")
        S.dma("sp", [(self.MATS.h[:], self.mats_d)], writes=[self.MATS.t()], semkey="c1")
        self.dve(I("tensor_copy", out=self.IDB.h[:], in_=self.MATS.h[:, 0:128]), [self.MATS.t()], [self.IDB.t()])
        self.dve(I("tensor_copy", out=self.ONB.h[:], in_=self.MATS.h[:, 256:384]), [self.MATS.t()], [self.ONB.t()])
        self.dve(I("tensor_copy", out=self.RTB.h[:], in_=self.MATS.h[:, 128:256]), [self.MATS.t()], [self.RTB.t()])
        self.act(I("activation", out=self.SIL.h[:], in_=self.SM.h[:, 0:32].rearrange("p (k c) -> p k c", c=2),
                                        func=AF.Silu), [self.SM.t()], [self.SIL.t()])
        self.ident = self.MATS.h[:, 0:128]
        self.rt = self.MATS.h[:, 128:256]
        self.ones = self.MATS.h[:, 256:384]
        self.ring_i = {"A": 0, "B": 0, "C": 0}
        self.rings = {"A": self.RA, "B": self.RB, "C": self.RC}

        self.mod_ring = "B"
        self.mod_bank = 7
        self.mod_gens = {}
        for li, l in enumerate(L):
            self.mod_gens[li] = self.mod_gen(li)
        self.pump(0, 16)
        qA = [(0, 256, 1), (256, 512, 0), (768, 512, 0)]
        qO = [(256, 512, 0), (768, 512, 0)]
        for li, l in enumerate(L):
            if l == 0:
                self.layer(li, l, qA, 0, xs_cols=(0, NQ), kv=("save" if nl == 2 else "compute"))
                if nl == 2:
                    self.layer(li, l, qO, 1024, xs_cols=(256, NQ), kv="load")
            else:
                self.layer(li, l, qO, 0)
        S.final_wait("sp", list(self.out_b.tiles.values()) + [b.t() for b in self.dbg_outs])
        S.emit()
        return nc

    def n_smalls(self):
        return 32 + len(self.layers) * 137 + 16

    def sm_off(self, li, what):
        base = 32 + li * 137
        offs = {"bmod": 0, "gmix": 96, "gmlp": 112, "gq": 128, "gkv": 132, "gsub": 136}
        return base + offs[what]

    def wload(self, ring, src3d, kc, ncols):
        bufs = self.rings[ring]
        i = self.ring_i[ring]
        self.ring_i[ring] = (i + 1) % len(bufs)
        b = bufs[i]
        view = b.h[:, 0:kc * ncols].rearrange("p (k n) -> p k n", k=kc)
        self.S.dma("pool", [(view, src3d)], writes=[b.t()], semkey=f"w{ring}{i}")
        return view, b.t()

    def wsrc(self, w, li, r0, kc, c0, ncols):
        return w[li, r0:r0 + kc * 128, c0:c0 + ncols].rearrange("(k p) n -> p k n", p=128)

    def mod_gen(self, li):
        MV = self.MODV[li]
        GS = self.GS[li]
        for blk in range(48):
            Pm = self.P[self.mod_bank]
            w, wt = self.wload(self.mod_ring, self.wsrc(self.w_mod, li, 0, 16, blk * 256, 256), 16, 256)
            for jj in range(2):
                j = blk * 2 + jj
                self.mm(Pm.h[:, 2 * j:2 * j + 2], Pm.t(),
                        [(w[:, k, jj * 128:(jj + 1) * 128], self.SIL.h[:, k, :]) for k in range(16)],
                        [wt, self.SIL.t()])
            s = blk // 8
            bo = self.sm_off(li, "bmod")
            self.dve(I("tensor_tensor",
                       out=MV.h[:, 2 * blk:2 * blk + 2, :],
                       in0=Pm.h[:, 4 * blk:4 * blk + 4].rearrange("p (j c) -> p j c", c=2),
                       in1=self.SM.h[:, bo + 2 * blk:bo + 2 * blk + 2].unsqueeze(2).broadcast_to([128, 2, 2]),
                       op=ALU.add), [Pm.t(), self.SM.t()], [MV.t(s)])
            if blk % 8 == 7:
                if s in (1, 4):
                    which = 0 if s == 1 else 1
                    go = self.sm_off(li, "gmix" if s == 1 else "gmlp")
                    self.dve(I("scalar_tensor_tensor",
                        out=GS.h[:, which, :, :], in0=MV.h[:, 16 * s:16 * s + 16, :], scalar=1.0,
                        in1=self.SM.h[:, go:go + 16].unsqueeze(2).broadcast_to([128, 16, 2]),
                        op0=ALU.add, op1=ALU.mult), [MV.t(s), self.SM.t()], [GS.t(which)])
            yield blk

    def pump(self, li, n):
        g = self.mod_gens.get(li)
        if g is None:
            return
        for _ in range(n):
            try:
                next(g)
            except StopIteration:
                self.mod_gens[li] = None
                return

    def modc(self, li, sec, j, c):
        return self.MODV[li].h[:, 16 * sec + j, c:c + 1]

    def rstd_from(self, ps, n, dst, scale):
        self.act(I("activation", out=dst.h[:, 0:n], in_=ps.h[:, 0:n], func=AF.Sqrt, bias=EPS, scale=scale),
                 [ps.t()], [dst.t()])
        self.dve(I("reciprocal", out=dst.h[:, 0:n], in_=dst.h[:, 0:n]), [dst.t()], [dst.t()])

    def rope_evac(self, ps, n, P_, lat0, out_ap, out_t, xc, xs, pr):
        tab = self.TAB
        self.dve(I("tensor_tensor", out=xc.h[0:P_, 0:n], in0=ps.h[0:P_, 0:n], in1=tab.h[0:P_, 0, lat0:lat0 + n],
                                           op=ALU.mult), [ps.t(), tab.t()], [xc.t()])
        self.dve(I("tensor_tensor", out=xs.h[0:P_, 0:n], in0=ps.h[0:P_, 0:n], in1=tab.h[0:P_, 1, lat0:lat0 + n],
                                           op=ALU.mult), [ps.t(), tab.t()], [xs.t()])

        def stage2():
            self.mm(pr.h[0:P_, 0:n], pr.t(), [(self.IDB.h[0:P_, 0:P_], xc.h[0:P_, 0:n]),
                                               (self.RTB.h[0:P_, 0:P_], xs.h[0:P_, 0:n])],
                    [self.IDB.t(), self.RTB.t(), xc.t(), xs.t()])
            self.act(I("copy", out=out_ap, in_=pr.h[0:P_, 0:n]), [pr.t()], [out_t])
        return stage2

    def transpose_to_tok(self, src, src_t, tok0, nchunk, dst, dst_tile_fn, pbank, c0=0):
        done = 0
        while done < nchunk:
            g = min(8, nchunk - done)
            pv = pbank.h[:].bitcast(BF16)
            for j in range(g):
                t0 = tok0 + (done + j) * 128
                self.pe(I("transpose", pv[:, j * 128:(j + 1) * 128], src.h[:, t0:t0 + 128],
                                                                 self.IDB.h[:]),
                        [src_t, self.IDB.t()], [pbank.t()])
            self.dve(I("tensor_copy",
                out=dst.h[:, c0 + done:c0 + done + g, :], in_=pv[:, 0:g * 128].rearrange("p (c f) -> p c f", f=128)),
                [pbank.t()], [dst_tile_fn(done)])
            done += g

    def layer(self, li, l, qblocks, ho, xs_cols=None, kv="compute"):
        S = self.S
        P = self.P
        nl = len(self.layers)
        upd_ctx = (qblocks[0][2] == 1)
        lam_init = 0.8 - 0.6 * math.exp(-0.3 * l)
        kblocks = [(0, 256, 1), (256, 512, 0), (768, 512, 0), (1280, 512, 0), (1792, 512, 0)]
        H = self.H

        S.dma("sp", [(self.BC.h[:], self.bc_d[li])], writes=[self.BC.t()], semkey="bc")
        S.dma("pool", [(self.WSPT.h[:], self.wspt_d[li])], writes=[self.WSPT.t()], semkey="wsp")
        lamp = self.BC.h[:, 1024:1280].rearrange("p (a d) -> p a d", d=64)
        self.dve(I("tensor_tensor", out=self.LTMP.h[:, 0, :], in0=lamp[:, 0, :], in1=lamp[:, 1, :], op=ALU.mult),
                 [self.BC.t()], [self.LTMP.t()])
        self.dve(I("tensor_tensor", out=self.LTMP.h[:, 1, :], in0=lamp[:, 2, :], in1=lamp[:, 3, :], op=ALU.mult),
                 [self.BC.t()], [self.LTMP.t()])
        self.dve(I("tensor_reduce", out=self.LAMV.h[:, 2:4], in_=self.LTMP.h[:], axis=AX.X, op=ALU.add),
                 [self.LTMP.t()], [self.LAMV.t()])
        self.act(I("activation", out=self.LAMV.h[:, 4:6], in_=self.LAMV.h[:, 2:4], func=AF.Exp),
                 [self.LAMV.t()], [self.LAMV.t()])
        self.dve(I("scalar_tensor_tensor", out=self.LAMV.h[:, 0:1], in0=self.LAMV.h[:, 5:6], scalar=-lam_init,
                                                  in1=self.LAMV.h[:, 4:5], op0=ALU.add, op1=ALU.subtract),
                 [self.LAMV.t()], [self.LAMV.t()])
        gso = self.sm_off(li, "gsub")
        self.dve(I("tensor_scalar", out=self.LAMV.h[:, 1:2], in0=self.SM.h[:, gso:gso + 1], scalar1=1.0 - lam_init,
                                           scalar2=0.0, op0=ALU.mult, op1=ALU.add), [self.SM.t()], [self.LAMV.t()])

        GS = self.GS[li]

        def n1_a1(seg):
            t0 = seg * 256
            xb = self.XSEG[seg % 3]
            if li == 0:
                src = self.xin[:, t0:t0 + 256].rearrange("(k p) t -> p k t", p=128)
                S.dma("sp", [(xb.h[:], src)], writes=[xb.t()], semkey=f"xseg{seg % 3}")
            else:
                src = self.XSD.h[:, t0:t0 + 256].rearrange("(k p) t -> p k t", p=128)
                S.dma("sp", [(xb.h[:], src)], reads=[self.XSD.t(0 if seg < 5 else 1)], writes=[xb.t()],
                      semkey=f"xseg{seg % 3}")
            SQ = self.SQb[seg % 2]
            self.act(I("activation", out=SQ.h[:], in_=xb.h[:], func=AF.Square), [xb.t()], [SQ.t()])
            ps = P[seg % 2]
            self.mm(ps.h[:, 0:256], ps.t(), [(self.ones, SQ.h[:, k, :]) for k in range(16)], [self.MATS.t(), SQ.t()])

        def n1_a2(seg):
            xb = self.XSEG[seg % 3]
            SQ = self.SQb[seg % 2]
            RS = self.RSb[seg % 2]
            self.rstd_from(P[seg % 2], 256, RS, 1.0 / D)
            self.dve(I("tensor_tensor", out=SQ.h[:], in0=xb.h[:],
                       in1=RS.h[:, 0:256].unsqueeze(1).broadcast_to([128, 16, 256]), op=ALU.mult),
                     [xb.t(), RS.t()], [SQ.t()])

        def n1_b(seg):
            t0 = seg * 256
            c = 1 if seg == 0 else 0
            SQ = self.SQb[seg % 2]
            for k in range(16):
                self.act(I("activation", out=H.h[:, k, t0:t0 + 256], in_=SQ.h[:, k, :], func=AF.Identity,
                           bias=self.modc(li, 0, k, c), scale=GS.h[:, 0, k, c:c + 1]),
                         [SQ.t(), self.MODV[li].t(0), GS.t(0)], [H.t(seg)])
        segs = list(range(9)) if kv != "load" else list(range((qblocks[0][0] + ho) // 256, 9))
        n1_a1(segs[0])
        n1_a2(segs[0])
        for si, seg in enumerate(segs):
            if si + 1 < len(segs):
                n1_a1(segs[si + 1])
            n1_b(seg)
            if si + 1 < len(segs):
                n1_a2(segs[si + 1])
        self.dump("h", H, H.h[:, :, 0:NQ], [H.t(s) for s in range(5)], [128, 16, NQ], BF16)

        def hts(t0, n):
            return [H.t(s) for s in range(t0 // 256, (t0 + n + 255) // 256)]

        S.dma("sp", [(self.TAB.h[:], self.tab_d)], writes=[self.TAB.t()], semkey="tab")

        HD = self.HD
        pj_i = [0]

        def pj():
            b = P[pj_i[0] % 4]
            pj_i[0] += 1
            return b

        def proj(w, wt, m0, M, t0, n, src=None, src_tiles=None, nk=16, hoff=0):
            ps = pj()
            if src is None:
                src = H
                t0 = t0 + hoff
                src_tiles = hts(t0, n)
            self.mm(ps.h[0:M, 0:n], ps.t(), [(w[:, k, m0:m0 + M], src.h[:, k, t0:t0 + n]) for k in range(nk)],
                    [wt] + src_tiles)
            return ps

        U, GVT = self.U, self.GVT
        for (t0, n, c) in qblocks:
            wgu = [self.wload("C", self.wsrc(self.w_in, li, 0, 16, C_GU + 256 * i, 256), 16, 256) for i in range(2)]
            for g in range(4):
                w, wt = wgu[g // 2]
                ps = proj(w, wt, (g % 2) * 128, 128, t0, n, hoff=ho)
                self.act(I("activation", out=U.h[:, g, 0:n], in_=ps.h[:, 0:n], func=AF.Gelu),
                         [ps.t()], [U.t(g)])
            wgv = [self.wload("C", self.wsrc(self.w_in, li, 0, 16, C_GV + 256 * i, 256), 16, 256) for i in range(2)]
            for g in range(4):
                w, wt = wgv[g // 2]
                ps = proj(w, wt, (g % 2) * 128, 128, t0, n, hoff=ho)
                self.act(I("activation", out=GVT.h[:, g, 0:n], in_=ps.h[:, 0:n], func=AF.Gelu),
                         [ps.t()], [GVT.t(g)])
            nch = n // 128
            pendg = []

            def flushg():
                for f in pendg:
                    f()
                del pendg[:]
            for m in range(nch):
                pt = P[4 + (m % 2)]
                for g in range(4):
                    self.pe(I("transpose", pt.h[:, g * 128:(g + 1) * 128],
                                                                    GVT.h[:, g, m * 128:(m + 1) * 128], self.ident),
                            [GVT.t(g), self.MATS.t()], [pt.t()])
                flushg()
                vf = self.VF[m % 2]
                self.act(I("copy", out=vf.h[:], in_=pt.h[:]), [pt.t()], [vf.t()])
                mx = self.MIXT
                self.dve(I("tensor_tensor", out=mx.h[:], in0=vf.h[:], in1=vf.h[:], op=ALU.mult),
                         [vf.t()], [mx.t()])
                gst = self.GST
                self.dve(I("tensor_reduce", out=gst.h[:, 0:4], in_=mx.h[:].rearrange("p (g c) -> p g c", g=4),
                                                   axis=AX.X, op=ALU.add), [mx.t()], [gst.t()])
                self.act(I("activation", out=gst.h[:, 4:8], in_=gst.h[:, 0:4], func=AF.Sqrt, bias=EPS,
                                                scale=1.0 / 128), [gst.t()], [gst.t()])
                self.dve(I("reciprocal", out=gst.h[:, 8:12], in_=gst.h[:, 4:8]), [gst.t()], [gst.t()])
                self.dve(I("tensor_tensor", out=vf.h[:].rearrange("p (g c) -> p g c", g=4),
                                                         in0=vf.h[:].rearrange("p (g c) -> p g c", g=4),
                                                         in1=gst.h[:, 8:12].unsqueeze(2).broadcast_to([128, 4, 128]),
                                                         op=ALU.mult), [vf.t(), gst.t()], [vf.t()])
                vt = self.VTOK[m % 2]
                self.dve(I("tensor_tensor", out=vt.h[:], in0=vf.h[:], in1=self.BC.h[:, 0:512],
                                                                op=ALU.mult), [vf.t(), self.BC.t()], [vt.t()])
                def mix(m=m, vt=vt, tt=t0 + m * 128):
                    pm = P[6]
                    mx2 = self.MIXT2
                    for g in range(4):
                        self.mm(pm.h[:, g * 128:(g + 1) * 128], pm.t(),
                                [(vt.h[:, g * 128:(g + 1) * 128], self.WSPT.h[:, g, :])], [vt.t(), self.WSPT.t()])
                    self.dve(I("tensor_tensor", out=mx2.h[:], in0=pm.h[:], in1=self.BC.h[:, 512:1024], op=ALU.add),
                             [pm.t(), self.BC.t()], [mx2.t()])
                    self.dve(I("tensor_tensor",
                               out=HD.h[:, 6:10, tt:tt + 128], in0=mx2.h[:].rearrange("p (g c) -> p g c", g=4),
                               in1=U.h[:, :, m * 128:(m + 1) * 128], op=ALU.mult),
                             [mx2.t()] + [U.t(g) for g in range(4)], [HD.t(("g", tt))])
                pendg.append(mix)
            flushg()
        self.dump("gated", HD, HD.h[:, 6:10, :], [HD.t(("g", tt)) for tt in range(0 if upd_ctx else 256, NQ, 128)],
                  [128, 4, NQ], BF16)

        QT, KT, VT, V = self.QT, self.KT, self.VT, self.V
        pend_epi = []
        pend_epb = []
        for h in range(6):
            wq = self.wload("C", self.wsrc(self.w_in, li, 0, 16, C_Q + 128 * h, 128), 16, 128)
            ri = 0
            pend = []

            def flush():
                for f in pend:
                    f()
                del pend[:]
            if kv == "load":
                S.dma("sp", [(KT.h[:], self.KVS.h[h, 0])], reads=[self.KVS.t(h)],
                      writes=[KT.t(t0) for (t0, n, c) in kblocks], semkey="kvl0")
                S.dma("sp", [(V.h[:], self.KVS.h[h, 1].rearrange("p (c f) -> p c f", f=128))], reads=[self.KVS.t(h)],
                      writes=[V.t(t0 // 128) for (t0, n, c) in kblocks], semkey="kvl1")
            else:
                wk = self.wload("C", self.wsrc(self.w_in, li, 0, 16, C_K + 128 * h, 128), 16, 128)
                wv = self.wload("C", self.wsrc(self.w_in, li, 0, 16, C_V + 128 * h, 128), 16, 128)
                for (t0, n, c) in kblocks:
                    ps = proj(wv[0], wv[1], 0, 128, t0, n)
                    self.act(I("copy", out=VT.h[:, t0:t0 + n], in_=ps.h[:, 0:n]), [ps.t()], [VT.t(t0)])
                for (t0, n, c) in kblocks:
                    ps = proj(wk[0], wk[1], 0, 128, t0, n)
                    flush()
                    if c == 1:
                        self.act(I("copy", out=KT.h[:, t0:t0 + n], in_=ps.h[:, 0:n]), [ps.t()], [KT.t(t0)])
                    else:
                        pend.append(self.rope_evac(ps, n, 128, t0 - 256, KT.h[:, t0:t0 + n], KT.t(t0), self.XC[ri % 2],
                                                   self.XS_[ri % 2], P[4 + ri % 2]))
                        ri += 1
                for (t0, n, c) in kblocks:
                    self.transpose_to_tok(VT, VT.t(t0), t0, n // 128, V, lambda d, t0=t0: V.t(t0 // 128), P[6], c0=t0 // 128)
            for (t0, n, c) in qblocks:
                ps = proj(wq[0], wq[1], 0, 128, t0, n, hoff=ho)
                flush()
                if c == 1:
                    self.act(I("copy", out=QT.h[:, t0:t0 + n], in_=ps.h[:, 0:n]), [ps.t()], [QT.t(t0)])
                else:
                    pend.append(self.rope_evac(ps, n, 128, t0 + ho - 256, QT.h[:, t0:t0 + n], QT.t(t0), self.XC[ri % 2],
                                               self.XS_[ri % 2], P[4 + ri % 2]))
                    ri += 1
            flush()
            kt_all = [KT.t(t0) for (t0, n, c) in kblocks]
            v_all = [V.t(t0 // 128) for (t0, n, c) in kblocks]
            if kv == "save":
                S.dma("sp", [(self.KVS.h[h, 0], KT.h[:]), (self.KVS.h[h, 1].rearrange("p (c f) -> p c f", f=128), V.h[:])],
                      reads=kt_all + v_all, writes=[self.KVS.t(h)], semkey="kvs")
            for (q0, nq, c) in qblocks:
                kcs = list(range(2)) if c == 1 else list(range(18))
                o1, o2, s1, s2 = P[4], P[5], P[6], P[7]
                nk = len(kcs)

                def s_mm(i):
                    kc = kcs[i]
                    pa, pb = P[(2 * i) % 4], P[(2 * i + 1) % 4]
                    self.mm(pa.h[:, 0:nq], pa.t(), [(KT.h[0:64, kc * 128:(kc + 1) * 128], QT.h[0:64, q0:q0 + nq])],
                            kt_all + [QT.t(q0)])
                    self.mm(pb.h[:, 0:nq], pb.t(), [(KT.h[64:128, kc * 128:(kc + 1) * 128], QT.h[64:128, q0:q0 + nq])],
                            kt_all + [QT.t(q0)])
                    return pa, pb
                pend = s_mm(0)
                for i in range(nk):
                    pa, pb = pend
                    ea, eb = self.ED[(2 * i) % 4], self.ED[(2 * i + 1) % 4]
                    self.act(I("activation", out=ea.h[:, 0:nq], in_=pa.h[:, 0:nq], func=AF.Exp, scale=DA_SCALE),
                             [pa.t()], [ea.t()])
                    self.act(I("activation", out=eb.h[:, 0:nq], in_=pb.h[:, 0:nq], func=AF.Exp, scale=DA_SCALE),
                             [pb.t()], [eb.t()])
                    if i + 1 < nk:
                        pend = s_mm(i + 1)
                    kc = kcs[i]
                    st, sp_ = (i == 0), (i == nk - 1)
                    for (ob, sb_, eb_) in ((o1, s1, ea), (o2, s2, eb)):
                        self.pe(I("matmul",
                            ob.h[:, 0:nq], lhsT=V.h[:, kc, :], rhs=eb_.h[:, 0:nq], start=st, stop=sp_),
                            v_all + [eb_.t()], [ob.t()])
                        self.pe(I("matmul",
                            sb_.h[:, 0:nq], lhsT=self.ONB.h[:], rhs=eb_.h[:, 0:nq], start=st, stop=sp_),
                            [self.ONB.t(), eb_.t()], [sb_.t()])
                    if i == min(1, nk - 1):
                        for f in pend_epb:
                            f()
                        del pend_epb[:]
                for f in pend_epi:
                    f()
                del pend_epi[:]
                T1, T2, T3, T4, T5 = self.EPD
                self.dve(I("reciprocal", out=T1.h[:, 0:nq], in_=s1.h[:, 0:nq]), [s1.t()], [T1.t()])
                self.act(I("copy", out=T4.h[:, 0:nq], in_=o1.h[:, 0:nq]), [o1.t()], [T4.t()])
                self.dve(I("reciprocal", out=T2.h[:, 0:nq], in_=s2.h[:, 0:nq]), [s2.t()], [T2.t()])
                self.act(I("copy", out=T5.h[:, 0:nq], in_=o2.h[:, 0:nq]), [o2.t()], [T5.t()])

                def epb(nq=nq, T1=T1, T2=T2, T4=T4, T5=T5):
                    self.dve(I("tensor_tensor", out=T1.h[:, 0:nq], in0=T4.h[:, 0:nq], in1=T1.h[:, 0:nq], op=ALU.mult),
                             [T4.t(), T1.t()], [T1.t()])
                    self.dve(I("tensor_tensor", out=T2.h[:, 0:nq], in0=T5.h[:, 0:nq], in1=T2.h[:, 0:nq], op=ALU.mult),
                             [T5.t(), T2.t()], [T2.t()])
                    self.dve(I("scalar_tensor_tensor", out=T1.h[:, 0:nq], in0=T2.h[:, 0:nq], scalar=self.LAMV.h[:, 0:1],
                               in1=T1.h[:, 0:nq], op0=ALU.mult, op1=ALU.add),
                             [T1.t(), T2.t(), self.LAMV.t()], [T1.t()])
                    self.act(I("activation", out=T4.h[:, 0:nq], in_=T1.h[:, 0:nq], func=AF.Square), [T1.t()], [T4.t()])
                pend_epb.append(epb)

                def epi2(h=h, q0=q0, nq=nq, T1=T1, T2=T4, T3=T3):
                    pst = P[0]
                    self.mm(pst.h[:, 0:nq], pst.t(), [(self.ones, T2.h[:, 0:nq])], [self.MATS.t(), T2.t()])
                    self.rstd_from(pst, nq, T3, 1.0 / 128)
                    self.dve(I("scalar_tensor_tensor",
                               out=HD.h[:, h, q0:q0 + nq], in0=T1.h[:, 0:nq], scalar=self.LAMV.h[:, 1:2], in1=T3.h[:, 0:nq],
                               op0=ALU.mult, op1=ALU.mult), [T1.t(), T3.t(), self.LAMV.t()], [HD.t((h, q0))])
                pend_epi.append(epi2)
        for f in pend_epb + pend_epi:
            f()
        del pend_epb[:]
        del pend_epi[:]
        self.dump("oda", HD, HD.h[:, 0:6, :], [HD.t((h, q0)) for h in range(6) for (q0, _, _) in qblocks], [128, 6, NQ], BF16)

        CQN, CKVN, KRT, CQF = self.CQN, self.CKVN, self.KRT, self.CQF
        for which in (("cq",) if kv == "load" else ("cq", "ckv")):
            col0 = C_CQ if which == "cq" else C_CKV
            blocks = qblocks if which == "cq" else kblocks
            dst = CQN if which == "cq" else CKVN
            go = self.sm_off(li, "gq" if which == "cq" else "gkv")
            for (t0, n, c) in blocks:
                ws = [self.wload("C", self.wsrc(self.w_in, li, 0, 16, col0 + 256 * i, 256), 16, 256) for i in range(2)]
                pstat = P[4]
                pst_pend = []
                for j in range(4):
                    w, wt = ws[j // 2]
                    ps = proj(w, wt, (j % 2) * 128, 128, t0, n, hoff=(ho if which == "cq" else 0))
                    for f in pst_pend:
                        f()
                    del pst_pend[:]
                    self.act(I("copy", out=CQF.h[:, j, 0:n], in_=ps.h[:, 0:n]), [ps.t()], [CQF.t(j)])
                    sq = self.SQT[j % 2]
                    self.act(I("activation", out=sq.h[:, 0:n], in_=ps.h[:, 0:n], func=AF.Square),
                             [ps.t()], [sq.t()])
                    pst_pend.append(lambda sq=sq, j=j, n=n, pstat=pstat: self.pe(
                        I("matmul", pstat.h[:, 0:n], lhsT=self.ones, rhs=sq.h[:, 0:n], start=(j == 0), stop=(j == 3)),
                        [self.MATS.t(), sq.t()], [pstat.t()]))
                for f in pst_pend:
                    f()
                del pst_pend[:]
                self.rstd_from(pstat, n, self.RSM, 1.0 / 512)
                for j in range(4):
                    self.dve(I("scalar_tensor_tensor",
                        out=dst.h[:, j, t0:t0 + n], in0=CQF.h[:, j, 0:n], scalar=self.SM.h[:, go + j:go + j + 1],
                        in1=self.RSM.h[:, 0:n], op0=ALU.mult, op1=ALU.mult),
                        [CQF.t(j), self.RSM.t(), self.SM.t()], [dst.t(t0)])
        if kv == "load":
            S.dma("sp", [(CKVN.h[:], self.MKS.h[0:4].rearrange("c p t -> p c t"))], reads=[self.MKS.t()],
                  writes=[CKVN.t(t0) for (t0, n, c) in kblocks], semkey="mkl0")
            S.dma("sp", [(KRT.h[0:64, :], self.MKS.h[4, 0:64, :])], reads=[self.MKS.t()],
                  writes=[KRT.t(t0) for (t0, n, c) in kblocks], semkey="mkl1")
        else:
            wkr = self.wload("C", self.wsrc(self.w_in, li, 0, 16, C_KR, 64), 16, 64)
            for (t0, n, c) in kblocks:
                ps = proj(wkr[0], wkr[1], 0, 64, t0, n)
                if c == 1:
                    self.act(I("copy", out=KRT.h[0:64, t0:t0 + n], in_=ps.h[0:64, 0:n]), [ps.t()], [KRT.t(t0)])
                else:
                    self.rope_evac(ps, n, 64, t0 - 256, KRT.h[0:64, t0:t0 + n], KRT.t(t0), self.XC2, self.XS2, P[5])()
            if kv == "save":
                S.dma("sp", [(self.MKS.h[0:4].rearrange("c p t -> p c t"), CKVN.h[:]), (self.MKS.h[4, 0:64, :], KRT.h[0:64, :])],
                      reads=[CKVN.t(t0) for (t0, n, c) in kblocks] + [KRT.t(t0) for (t0, n, c) in kblocks],
                      writes=[self.MKS.t()], semkey="mks")
        self.dump("cqn", CQN, CQN.h[:], [CQN.t(t0) for (t0, _, _) in qblocks], [128, 4, NQ], BF16)

        QN, QR, KN, VMT, VM = self.QN, self.QR, self.KN, self.VMT, self.VM
        cqn_all = [CQN.t(t0) for (t0, _, _) in qblocks]
        ckvn_all = [CKVN.t(t0) for (t0, _, _) in kblocks]
        krt_all = [KRT.t(t0) for (t0, _, _) in kblocks]
        for h in range(6):
            wuq = self.wload("C", self.wsrc(self.w_uq, li, 0, 4, 192 * h, 192), 4, 192)
            wukv = self.wload("C", self.wsrc(self.w_ukv, li, 0, 4, 256 * h, 256), 4, 256)
            ri = 0
            pendm = []

            def flushm():
                for f in pendm:
                    f()
                del pendm[:]
            for (t0, n, c) in qblocks:
                ps = proj(wuq[0], wuq[1], 0, 128, t0, n, src=CQN, src_tiles=[CQN.t(t0)], nk=4)
                flushm()
                self.act(I("copy", out=QN.h[:, t0:t0 + n], in_=ps.h[:, 0:n]), [ps.t()], [QN.t(t0)])
                ps = proj(wuq[0], wuq[1], 128, 64, t0, n, src=CQN, src_tiles=[CQN.t(t0)], nk=4)
                if c == 1:
                    self.act(I("copy", out=QR.h[0:64, t0:t0 + n], in_=ps.h[0:64, 0:n]), [ps.t()], [QR.t(t0)])
                else:
                    pendm.append(self.rope_evac(ps, n, 64, t0 + ho - 256, QR.h[0:64, t0:t0 + n], QR.t(t0), self.XC2, self.XS2, P[5]))
            for (t0, n, c) in kblocks:
                ps = proj(wukv[0], wukv[1], 0, 128, t0, n, src=CKVN, src_tiles=[CKVN.t(t0)], nk=4)
                flushm()
                self.act(I("copy", out=KN.h[:, t0:t0 + n], in_=ps.h[:, 0:n]), [ps.t()], [KN.t(t0)])
                ps = proj(wukv[0], wukv[1], 128, 128, t0, n, src=CKVN, src_tiles=[CKVN.t(t0)], nk=4)
                self.act(I("copy", out=VMT.h[:, t0:t0 + n], in_=ps.h[:, 0:n]), [ps.t()], [VMT.t(t0)])
            for (t0, n, c) in kblocks:
                self.transpose_to_tok(VMT, VMT.t(t0), t0, n // 128, VM, lambda d, t0=t0: VM.t(t0 // 128), P[6], c0=t0 // 128)
            kn_all = [KN.t(t0) for (t0, n, c) in kblocks]
            vm_all = [VM.t(t0 // 128) for (t0, n, c) in kblocks]
            for (q0, nq, c) in qblocks:
                kcs = list(range(2)) if c == 1 else list(range(18))
                ob, sb_ = P[4], P[7]
                nk = len(kcs)

                def s_mm(i):
                    kc = kcs[i]
                    pa = P[i % 4]
                    self.mm(pa.h[:, 0:nq], pa.t(),
                            [(KN.h[:, kc * 128:(kc + 1) * 128], QN.h[:, q0:q0 + nq]),
                             (KRT.h[0:64, kc * 128:(kc + 1) * 128], QR.h[0:64, q0:q0 + nq])],
                            kn_all + krt_all + [QN.t(q0), QR.t(q0)])
                    return pa
                pend = s_mm(0)
                for i in range(nk):
                    pa = pend
                    ea = self.EM[i % 4]
                    self.act(I("activation", out=ea.h[:, 0:nq], in_=pa.h[:, 0:nq], func=AF.Exp, scale=MLA_SCALE),
                             [pa.t()], [ea.t()])
                    if i + 1 < nk:
                        pend = s_mm(i + 1)
                    kc = kcs[i]
                    st, sp_ = (i == 0), (i == nk - 1)
                    self.pe(I("matmul",
                        ob.h[:, 0:nq], lhsT=VM.h[:, kc, :], rhs=ea.h[:, 0:nq], start=st, stop=sp_),
                        vm_all + [ea.t()], [ob.t()])
                    self.pe(I("matmul",
                        sb_.h[:, 0:nq], lhsT=self.ONB.h[:], rhs=ea.h[:, 0:nq], start=st, stop=sp_),
                        [self.ONB.t(), ea.t()], [sb_.t()])
                    if i % 6 == 5:
                        self.mod_ring, self.mod_bank = "C", 6
                        self.pump(li, 1)
                        self.mod_ring, self.mod_bank = "B", 7
                T1, T2 = self.EPM
                self.dve(I("reciprocal", out=T1.h[:, 0:nq], in_=sb_.h[:, 0:nq]), [sb_.t()], [T1.t()])
                self.act(I("copy", out=T2.h[:, 0:nq], in_=ob.h[:, 0:nq]), [ob.t()], [T2.t()])
                self.dve(I("tensor_tensor", out=HD.h[:, 10 + h, q0:q0 + nq], in0=T2.h[:, 0:nq],
                                                                       in1=T1.h[:, 0:nq], op=ALU.mult),
                         [T2.t(), T1.t()], [HD.t((10 + h, q0))])
        self.mod_ring, self.mod_bank = "C", 6
        self.pump(li, 48)
        self.mod_ring, self.mod_bank = "B", 7
        self.dump("heads", HD, HD.h[:], [HD.t((10 + h, q0)) for h in range(6) for (q0, _, _) in qblocks], [128, 16, NQ], BF16)

        X = self.X
        hd_all = list(HD.tiles.values())
        tq0 = qblocks[0][0]
        for jb in range(8):
            w, wt = self.wload("A", self.wsrc(self.w_out, li, 0, 16, 256 * jb, 256), 16, 256)
            for jj in range(2):
                j = 2 * jb + jj
                xo = self.XOLD[j % 2]
                if li == 0:
                    src = self.xin[j * 128:(j + 1) * 128, tq0 + ho:NQ + ho]
                else:
                    src = self.XSD.h[j * 128:(j + 1) * 128, tq0 + ho:NQ + ho]
                S.dma("sp", [(xo.h[:, tq0:NQ], src)], reads=([self.XSD.t(0)] if li else []), writes=[xo.t()], semkey=f"xold{j % 2}")
                for bi, (t0, n, c) in enumerate(qblocks):
                    ps = P[(j * 3 + bi) % 6]
                    self.mm(ps.h[:, 0:n], ps.t(), [(w[:, k, jj * 128:(jj + 1) * 128], HD.h[:, k, t0:t0 + n]) for k in range(16)],
                            [wt] + hd_all)
                    self.dve(I("scalar_tensor_tensor",
                        out=X.h[:, j, t0:t0 + n], in0=ps.h[:, 0:n], scalar=self.modc(li, 2, j, c), in1=xo.h[:, t0:t0 + n],
                        op0=ALU.mult, op1=ALU.add), [ps.t(), xo.t(), self.MODV[li].t(2)], [X.t((j, t0))])
        self.dump("xmix", X, X.h[:], list(X.tiles.values()), [128, 16, NQ])

        H2, AJ = self.H2, self.AJ
        GS = self.GS[li]
        for bi, (t0, n, c) in enumerate(qblocks):
            pst = P[6]
            prev = None
            for k in range(17):
                if k < 16:
                    sq = self.SQ2[k % 3]
                    self.act(I("activation", out=sq.h[:, 0:n], in_=X.h[:, k, t0:t0 + n], func=AF.Square),
                             [X.t((k, t0))], [sq.t()])
                if prev is not None:
                    pk, psq = prev
                    self.pe(I("matmul", pst.h[:, 0:n], lhsT=self.ones, rhs=psq.h[:, 0:n], start=(pk == 0), stop=(pk == 15)),
                            [self.MATS.t(), psq.t()], [pst.t()])
                prev = (k, sq) if k < 16 else None
            self.rstd_from(pst, n, self.RS2, 1.0 / D)
            for k in range(16):
                tmp = self.N2T[k % 2]
                self.dve(I("scalar_tensor_tensor",
                    out=tmp.h[:, 0:n], in0=X.h[:, k, t0:t0 + n], scalar=GS.h[:, 1, k, c:c + 1], in1=self.RS2.h[:, 0:n],
                    op0=ALU.mult, op1=ALU.mult), [X.t((k, t0)), GS.t(1), self.RS2.t()], [tmp.t()])
                self.act(I("activation",
                    out=H2.h[:, k, t0:t0 + n], in_=tmp.h[:, 0:n], func=AF.Identity, bias=self.modc(li, 3, k, c), scale=1.0),
                    [tmp.t(), self.MODV[li].t(3)], [H2.t(t0)])
        h2_all = [H2.t(t0) for (t0, _, _) in qblocks]
        self.dump("h2", H2, H2.h[:], h2_all, [128, 16, NQ], BF16)
        nxt = li + 1 if li + 1 < nl else None
        f1i = 0
        for jb in range(8):
            for sub in range(4):
                w, wt = self.wload("B", self.wsrc(self.w_fc1, li, 0, 16, jb * 1024 + sub * 256, 256), 16, 256)
                for jj in range(2):
                    hc = sub * 2 + jj
                    for (t0, n, c) in qblocks:
                        ps = P[f1i % 3]
                        f1i += 1
                        self.mm(ps.h[:, 0:n], ps.t(), [(w[:, k, jj * 128:(jj + 1) * 128], H2.h[:, k, t0:t0 + n]) for k in range(16)],
                                [wt, H2.t(t0)])
                        tr = self.TMPR[f1i % 2]
                        self.act(I("activation", out=tr.h[:, 0:n], in_=ps.h[:, 0:n], func=AF.Relu),
                                 [ps.t()], [tr.t()])
                        self.dve(I("tensor_tensor", out=AJ.h[:, hc, t0:t0 + n], in0=tr.h[:, 0:n],
                                                                                    in1=tr.h[:, 0:n], op=ALU.mult),
                                 [tr.t()], [AJ.t((hc, t0))])
                if nxt is not None:
                    self.pump(nxt, 1)
            aj_all = list(AJ.tiles.values())
            for ob in range(8):
                w, wt = self.wload("B", self.wsrc(self.w_fc2, li, jb * 1024, 8, ob * 256, 256), 8, 256)
                for jj in range(2):
                    j = ob * 2 + jj
                    for (t0, n, c) in qblocks:
                        ps = P[3 + f1i % 3]
                        f1i += 1
                        self.mm(ps.h[:, 0:n], ps.t(), [(w[:, k, jj * 128:(jj + 1) * 128], AJ.h[:, k, t0:t0 + n]) for k in range(8)],
                                [wt] + aj_all)
                        self.dve(I("scalar_tensor_tensor",
                            out=X.h[:, j, t0:t0 + n], in0=ps.h[:, 0:n], scalar=self.modc(li, 5, j, c), in1=X.h[:, j, t0:t0 + n],
                            op0=ALU.mult, op1=ALU.add), [ps.t(), X.t((j, t0)), self.MODV[li].t(5)], [X.t((j, t0))])
                if nxt is not None and ob % 4 == 3:
                    self.pump(nxt, 1)
        if nxt is not None:
            self.pump(nxt, 48)
        x_all = list(X.tiles.values())
        self.dump("xout", X, X.h[:], x_all, [128, 16, NQ])

        if l == 1:
            gfo = 32 + nl * 137
            for bi, (t0, n, c) in enumerate(qblocks):
                pst = P[bi % 2]
                for k in range(16):
                    sq = self.SQ2[k % 3]
                    self.act(I("activation", out=sq.h[:, 0:n], in_=X.h[:, k, t0:t0 + n], func=AF.Square),
                             [X.t((k, t0))], [sq.t()])
                    self.pe(I("matmul", pst.h[:, 0:n], lhsT=self.ones, rhs=sq.h[:, 0:n],
                                                                      start=(k == 0), stop=(k == 15)),
                            [self.MATS.t(), sq.t()], [pst.t()])
                self.rstd_from(pst, n, self.RS2, 1.0 / D)
                for k in range(16):
                    ost = self.OST[k % 2]
                    self.dve(I("scalar_tensor_tensor",
                        out=ost.h[:, 0:n], in0=X.h[:, k, t0:t0 + n], scalar=self.SM.h[:, gfo + k:gfo + k + 1], in1=self.RS2.h[:, 0:n],
                        op0=ALU.mult, op1=ALU.mult), [X.t((k, t0)), self.SM.t(), self.RS2.t()], [ost.t()])
                    S.dma("sp", [(self.out_b.h[k * 128:(k + 1) * 128, t0 - 256:t0 - 256 + n], ost.h[:, 0:n])],
                          reads=[ost.t()], writes=[self.out_b.t((k, t0))], semkey=f"ost{k % 2}")
        else:
            lo, hi = xs_cols
            if nl == 1:
                S.dma("sp", [(self.out_b.h.rearrange("(k p) t -> p k t", p=128), X.h[:])], reads=x_all,
                      writes=[self.out_b.t()], semkey="xs_w")
            else:
                S.dma("sp", [(self.XSD.h[:, lo + ho:hi + ho].rearrange("(k p) t -> p k t", p=128), X.h[:, :, lo:hi])],
                      reads=x_all, writes=[self.XSD.t(0 if ho == 0 else 1)], semkey="xs_w")


def _rope_tables():
    rows = 2048 // 64
    row = np.repeat(np.arange(rows, dtype=np.float32), 64)
    col = np.tile(np.arange(64, dtype=np.float32), rows)
    n_f = 16
    inv = (np.float32(10000.0) ** (-np.arange(n_f, dtype=np.float32) / np.float32(n_f))).astype(np.float32)
    ang = np.concatenate([row[:, None] * inv, col[:, None] * inv], axis=-1).astype(np.float32)
    return np.cos(ang).astype(np.float32), np.sin(ang).astype(np.float32)


def _fm(v):
    return np.ascontiguousarray(v.reshape(-1, 128).T)


def _consts():
    ident = np.eye(128, dtype=np.float32)
    RT = np.zeros((128, 128), np.float32)
    for i in range(64):
        RT[2 * i + 1, 2 * i] = -1.0
        RT[2 * i, 2 * i + 1] = 1.0
    return np.concatenate([ident, RT, np.ones((128, 128), np.float32)], 1)


def _core_inputs(inp, layers, core, xin):
    b, s = core // 2, core % 2
    nl = len(layers)
    cos, sin = _rope_tables()
    order = np.concatenate([np.arange(s * 1024, (s + 1) * 1024), np.arange((1 - s) * 1024, (2 - s) * 1024)])
    pidx = (np.arange(128) % 64) // 2
    tab = np.stack([cos[order][:, pidx].T, sin[order][:, pidx].T], axis=1).astype(np.float32)
    sm = [np.stack([_fm(inp["c"][b]), _fm(inp["c_ctx"])], axis=2).reshape(128, 32)]
    for l in layers:
        sm += [_fm(inp["b_mod"][l]), _fm(inp["g_norm_mix"][l]), _fm(inp["g_norm_mlp"][l]), _fm(inp["g_mla_q"][l]),
               _fm(inp["g_mla_kv"][l]), _fm(inp["g_da_sub"][l])]
    sm.append(_fm(inp["g_final"]))
    smalls = np.ascontiguousarray(np.concatenate(sm, axis=1).astype(np.float32))
    wspt = np.stack([np.transpose(inp["w_spatial"][l], (2, 0, 1)) for l in layers]).astype(np.float32)
    bc = []
    for l in layers:
        row = np.concatenate([inp["g_gm_v"][l].reshape(512), inp["b_spatial"][l].reshape(512),
                              inp["lam_q1"][l], inp["lam_k1"][l], inp["lam_q2"][l], inp["lam_k2"][l]]).astype(np.float32)
        bc.append(np.broadcast_to(row[None, :], (128, 1280)))
    bc = np.ascontiguousarray(np.stack(bc))
    d = dict(xin=xin, smalls=smalls, mats=_consts(), ropetab=np.ascontiguousarray(tab),
             wspt=np.ascontiguousarray(wspt), bcast=bc)
    return d


def _weights(inp, layers):
    sl = slice(layers[0], layers[-1] + 1)
    return {k: np.ascontiguousarray(inp[k][sl]) for k in
            ("w_mod", "w_in", "w_mla_uq", "w_mla_ukv", "w_out", "w_fc1", "w_fc2")}


_PROG_CACHE = {}


def _get_prog(layers, fused, dbg=()):
    key = (tuple(layers), fused, tuple(dbg))
    if key not in _PROG_CACHE:
        p = Prog(list(layers), fused, dbg)
        p.build()
        _PROG_CACHE[key] = p
    return _PROG_CACHE[key]


def _xin_layer0(inp, core):
    b, s = core // 2, core % 2
    x = inp["x"][b]
    return np.ascontiguousarray(np.concatenate(
        [inp["ctx"][b].T, x[s * 1024:(s + 1) * 1024].T, x[(1 - s) * 1024:(2 - s) * 1024].T], axis=1).astype(np.float32))


def run_layers(inp, layers, xins, dbg=(), cores=range(8)):
    p = _get_prog(layers, False, dbg)
    w = _weights(inp, layers)
    in_maps = []
    for ci, core in enumerate(cores):
        d = _core_inputs(inp, layers, core, xins[ci])
        d.update(w)
        in_maps.append(d)
    res = run_bass_kernel_spmd(p.nc, in_maps, core_ids=list(range(len(in_maps))))
    return res.results


def kernel(**inp):
    inp = {k: np.asarray(v) for k, v in inp.items()}
    xin0 = [_xin_layer0(inp, c) for c in range(8)]
    res = run_layers(inp, [0, 1], xin0)
    out = np.empty((4, 2048, 2048), np.float32)
    for c in range(8):
        b, s = c // 2, c % 2
        out[b, s * 1024:(s + 1) * 1024, :] = res[c]["outT"].T
    return out
```
